# Optimizing a Trainium2 kernel written in Bass

```python
import jax, jax.numpy as jnp
from jax import lax
import numpy as np

D_MODEL = 2048
BATCH = 4
SEQ = 4096
DEPTH = 2

CHUNK = 64
D_RNN = 2048
LRU_BLOCKS = 8
LRU_BLOCK_W = D_RNN // LRU_BLOCKS
LRU_C = 8.0
CONV_A_WIDTH = 4
D_CONV = 2048
CONV_B_WIDTH = 3
D_FF = 4 * D_MODEL
EPS = 1e-6

SPLIT_SIZES = (D_RNN, D_RNN, D_CONV, D_CONV, D_CONV, D_MODEL, D_MODEL)
SPLIT_POINTS = tuple(int(v) for v in np.cumsum(SPLIT_SIZES)[:-1])
N_IN = int(sum(SPLIT_SIZES))

kernel_name = "hybrid_rglru_shortconv_gated_trunk"


def _rmsnorm(x, g):
    xf = x.astype(jnp.float32)
    y = xf * lax.rsqrt(jnp.mean(xf * xf, axis=-1, keepdims=True) + EPS)
    return (y * g.astype(jnp.float32)).astype(x.dtype)


def _causal_dwconv(x, w):
    k, c = w.shape
    return lax.conv_general_dilated(
        x, w[:, None, :].astype(x.dtype), window_strides=(1,), padding=[(k - 1, 0)],
        dimension_numbers=("NWC", "WIO", "NWC"), feature_group_count=c)


def _lru_combine(left, right):
    a_l, b_l = left
    a_r, b_r = right
    return a_l * a_r, a_r * b_l + b_r


def _rg_lru(x, wr, br, wi, bi, lam):
    bsz, s, c = x.shape
    xb = x.reshape(bsz, s, LRU_BLOCKS, LRU_BLOCK_W)
    r = jax.nn.sigmoid(jnp.einsum("bsnh,nhk->bsnk", xb, wr) + br).reshape(bsz, s, c)
    i = jax.nn.sigmoid(jnp.einsum("bsnh,nhk->bsnk", xb, wi) + bi).reshape(bsz, s, c)
    log_a = -LRU_C * r.astype(jnp.float32) * jax.nn.softplus(-lam.astype(jnp.float32))
    a = jnp.exp(log_a)
    mult = jnp.sqrt(-jnp.expm1(2.0 * log_a))
    b = mult * (i * x).astype(jnp.float32)
    _, h = lax.associative_scan(_lru_combine, (a, b), axis=1)
    return h.astype(x.dtype)


def _layer(x, g1, w_in, b_in, conv_a_w, conv_a_b, lru_wr, lru_br, lru_wi, lru_bi, lru_lam,
           conv_b_w, w_pa, w_pb, w_o, g2, w_mlp1, w_mlp2):
    h = _rmsnorm(x, g1)
    z = jnp.einsum("bsd,dn->bsn", h, w_in) + b_in
    xa, ya, cb, cc, cx, ga, gb = jnp.split(z, SPLIT_POINTS, axis=-1)
    xa = _causal_dwconv(xa, conv_a_w) + conv_a_b
    xa = _rg_lru(xa, lru_wr, lru_br, lru_wi, lru_bi, lru_lam)
    out_a = jnp.einsum("bsc,cd->bsd", xa * jax.nn.gelu(ya), w_pa)
    out_b = jnp.einsum("bsc,cd->bsd", cb * _causal_dwconv(cc * cx, conv_b_w), w_pb)
    merged = jax.nn.sigmoid(ga) * out_a + jax.nn.sigmoid(gb) * out_b
    x = x + jnp.einsum("bsd,de->bse", merged, w_o)
    h2 = _rmsnorm(x, g2)
    u = jnp.square(jax.nn.relu(jnp.einsum("bsd,df->bsf", h2, w_mlp1)))
    return x + jnp.einsum("bsf,fd->bsd", u, w_mlp2)


def setup_inputs(seed: int = 0) -> dict:
    key = jax.random.key(seed)
    ks = jax.random.split(key, 24)
    f32 = jnp.float32
    L = DEPTH

    def nrm(k, shape, scale):
        return jax.random.normal(k, shape, f32) * scale

    u = jax.random.uniform(ks[10], (L, D_RNN), f32, minval=0.9, maxval=0.999)
    p = u ** (1.0 / LRU_C)
    lru_lam = jnp.log(p) - jnp.log1p(-p)
    return {
        "x": nrm(ks[0], (BATCH, SEQ, D_MODEL), 1.0),
        "norm1_g": 1.0 + nrm(ks[1], (L, D_MODEL), 0.02),
        "w_in": nrm(ks[2], (L, D_MODEL, N_IN), D_MODEL ** -0.5),
        "b_in": nrm(ks[3], (L, N_IN), 0.02),
        "conv_a_w": nrm(ks[4], (L, CONV_A_WIDTH, D_RNN), CONV_A_WIDTH ** -0.5),
        "conv_a_b": nrm(ks[5], (L, D_RNN), 0.02),
        "lru_wr": nrm(ks[6], (L, LRU_BLOCKS, LRU_BLOCK_W, LRU_BLOCK_W), LRU_BLOCK_W ** -0.5),
        "lru_br": nrm(ks[7], (L, LRU_BLOCKS, LRU_BLOCK_W), 0.02),
        "lru_wi": nrm(ks[8], (L, LRU_BLOCKS, LRU_BLOCK_W, LRU_BLOCK_W), LRU_BLOCK_W ** -0.5),
        "lru_bi": nrm(ks[9], (L, LRU_BLOCKS, LRU_BLOCK_W), 0.02),
        "lru_lam": lru_lam,
        "conv_b_w": nrm(ks[11], (L, CONV_B_WIDTH, D_CONV), CONV_B_WIDTH ** -0.5),
        "w_pa": nrm(ks[12], (L, D_RNN, D_MODEL), D_RNN ** -0.5),
        "w_pb": nrm(ks[13], (L, D_CONV, D_MODEL), D_CONV ** -0.5),
        "w_o": nrm(ks[14], (L, D_MODEL, D_MODEL), D_MODEL ** -0.5),
        "norm2_g": 1.0 + nrm(ks[15], (L, D_MODEL), 0.02),
        "w_mlp1": nrm(ks[16], (L, D_MODEL, D_FF), D_MODEL ** -0.5),
        "w_mlp2": nrm(ks[17], (L, D_FF, D_MODEL), D_FF ** -0.5),
        "final_g": 1.0 + nrm(ks[18], (D_MODEL,), 0.02),
    }


def reference(x, norm1_g, w_in, b_in, conv_a_w, conv_a_b, lru_wr, lru_br, lru_wi, lru_bi,
              lru_lam, conv_b_w, w_pa, w_pb, w_o, norm2_g, w_mlp1, w_mlp2, final_g):
    for l in range(DEPTH):
        x = _layer(x, norm1_g[l], w_in[l], b_in[l], conv_a_w[l], conv_a_b[l], lru_wr[l], lru_br[l],
                   lru_wi[l], lru_bi[l], lru_lam[l], conv_b_w[l], w_pa[l], w_pb[l], w_o[l],
                   norm2_g[l], w_mlp1[l], w_mlp2[l])
    return _rmsnorm(x, final_g)
```

```python
import contextlib
import numpy as np
import concourse.bass as bass
import concourse.mybir as mybir
from concourse.bass_utils import run_bass_kernel_spmd

F32 = mybir.dt.float32
BF16 = mybir.dt.bfloat16
AF = mybir.ActivationFunctionType
ALU = mybir.AluOpType

D = 2048
NCH = 16
T = 512
DEPTH = 2
N_IN = 14336
D_FF = 8192
EPS = 1e-6
NSLOT = 4
NTMP = 18
NTB = 5
TMPW = 516

_PAR = {}
_off = 0
for _l in range(DEPTH):
    for _name, _n in (("g1", 16), ("bin", 112), ("caw", 64), ("cab", 16), ("br", 16), ("bi", 16),
                      ("lam", 16), ("cbw", 48), ("g2", 16)):
        _PAR[(_name, _l)] = _off
        _off += _n
_PAR[("gf", 0)] = _off
_off += 16
NS = 9
for _name in ("fr", "keep", "m0", "m1"):
    _PAR[(_name, 0)] = _off
    _off += NS
NPAR = _off
NST = 96
_DER = {}
_off = 0
for _l in range(DEPTH):
    for _name, _n in (("hbr", 16), ("hbi", 16), ("hbg", 32), ("hls", 16), ("e", 16)):
        _DER[(_name, _l)] = _off
        _off += _n
NDER = _off


class Sched:
    def __init__(self, nc, stack):
        self.nc = nc
        self.stack = stack
        self.q = {k: [] for k in ("pe", "act", "dve", "pool", "sp")}
        self.sems = {}
        self.cnt = {}
        self.hw = {k: {} for k in self.q}
        self.res = {}
        self.own = {}
        for k in ("pe", "act", "dve", "pool"):
            self.own[k] = self.new_sem("s_" + k)

    def new_sem(self, name):
        h = self.stack.enter_context(self.nc.semaphore(name))
        self.sems[name] = h
        self.cnt[name] = 0
        return name

    def op(self, q, fns, reads=(), writes=(), dma=None, sem=None):
        own = self.own.get(q)
        need = {}

        def want(tok, hazard_same_ok):
            if tok is None:
                return
            s, v = tok
            if hazard_same_ok and s == own:
                return
            if need.get(s, 0) < v:
                need[s] = v

        for r in reads:
            st = self.res.get(r)
            if st is not None:
                want(st[0], False)
        for w in writes:
            st = self.res.get(w)
            if st is not None:
                want(st[0], True)
                for t in st[1]:
                    want(t, True)
        waits = []
        hw = self.hw[q]
        for s, v in need.items():
            if hw.get(s, 0) < v:
                hw[s] = v
                waits.append((s, v))
        if sem is not None:
            self.cnt[sem] += 1
            tok = (sem, self.cnt[sem])
            inc = (sem, None)
        elif dma is not None:
            self.cnt[dma] += 16
            tok = (dma, self.cnt[dma])
            inc = (dma, 16)
        else:
            self.cnt[own] += 1
            tok = (own, self.cnt[own])
            inc = (own, 1)
        self.q[q].append((waits, fns, inc))
        for r in reads:
            st = self.res.get(r)
            if st is None:
                st = self.res[r] = [None, []]
            st[1].append(tok)
        for w in writes:
            self.res[w] = [tok, []]
        return tok

    def final_wait(self, q, toks):
        waits = []
        for s, v in toks:
            waits.append((s, v))
        self.q[q].append((waits, [], None))

    def replay(self, q, eng):
        for waits, fns, inc in self.q[q]:
            for s, v in waits:
                eng.wait_ge(self.sems[s], v)
            last = None
            for f in fns:
                last = f(eng)
            if inc is not None and last is not None:
                if inc[1] is None:
                    last.then_inc(self.sems[inc[0]])
                else:
                    last.then_inc(self.sems[inc[0]], inc[1])


class Pool_:
    def __init__(self, n):
        self.free = list(range(n))

    def get(self):
        return self.free.pop(0)

    def put(self, i):
        self.free.append(i)


def build_program(debug=False, ns=NS, n_pairs=4):
    nc = bass.Bass("TRN2", target_bir_lowering=False)
    stack = contextlib.ExitStack()
    with stack:
        xT = nc.dram_tensor("xT", [D, ns * T], F32, kind="ExternalInput").ap()
        par_d = nc.dram_tensor("par", [128, NPAR], F32, kind="ExternalInput").ap()
        w_in = nc.dram_tensor("w_in", [DEPTH, D, N_IN], F32, kind="ExternalInput").ap()
        lru_wr = nc.dram_tensor("lru_wr", [DEPTH, 8, 256, 256], F32, kind="ExternalInput").ap()
        lru_wi = nc.dram_tensor("lru_wi", [DEPTH, 8, 256, 256], F32, kind="ExternalInput").ap()
        w_pa = nc.dram_tensor("w_pa", [DEPTH, D, D], F32, kind="ExternalInput").ap()
        w_pb = nc.dram_tensor("w_pb", [DEPTH, D, D], F32, kind="ExternalInput").ap()
        w_o = nc.dram_tensor("w_o", [DEPTH, D, D], F32, kind="ExternalInput").ap()
        w_mlp1 = nc.dram_tensor("w_mlp1", [DEPTH, D, D_FF], F32, kind="ExternalInput").ap()
        w_mlp2 = nc.dram_tensor("w_mlp2", [DEPTH, D_FF, D], F32, kind="ExternalInput").ap()
        outT = nc.dram_tensor("outT", [D, ns * T], F32, kind="ExternalOutput").ap()
        scr1 = nc.dram_tensor("scr1", [DEPTH, 16, 128, 16 * 512], BF16).ap()
        scr2 = nc.dram_tensor("scr2", [DEPTH, 16, 128, 16 * 512], BF16).ap()
        cin_d = [nc.dram_tensor(f"cin{i}", [128, NST], F32).ap() for i in range(ns - 1)]
        cout_d = [nc.dram_tensor(f"cout{i}", [256, NST], F32).ap() for i in range(ns - 1)]
        dbg = nc.dram_tensor("dbg", [8, D, T], F32, kind="ExternalOutput").ap() if debug else None

        def sb(name, shape, dt):
            return stack.enter_context(nc.sbuf_tensor(name, shape, dt))

        xres = sb("xres", [128, NCH, T], F32)
        hnA = sb("hnA", [128, NCH, T], BF16)
        hnB = sb("hnB", [128, NCH, T], BF16)
        U = sb("U", [128, 32, T], BF16)
        wsl = [sb(f"wsl{i}", [128, 16, 512], BF16) for i in range(NSLOT)]
        tmp = [sb(f"tmp{i}", [128, TMPW], F32) for i in range(NTMP)]
        tb = [sb(f"tb{i}", [128, T], BF16) for i in range(NTB)]
        par = sb("par_s", [128, NPAR], F32)
        der = sb("der_s", [128, NDER], F32)
        car_in = [sb(f"car_in{i}", [128, NST], F32) for i in range(2)]
        car_out = [sb(f"car_out{i}", [128, NST], F32) for i in range(2)]
        rcv = sb("rcv", [128, 2, NST], F32)
        hnBst = hnB[:].bitcast(F32).rearrange("p (c two) t -> p c (two t)", two=2)
        ones = sb("ones", [128, 128], BF16)
        ps = stack.enter_context(nc.psum_tensor("ps", [128, 8, 512], F32))

        S = Sched(nc, stack)
        XRES_ = [("xres", c) for c in range(NCH)]
        wsem = [S.new_sem(f"s_w{i}") for i in range(NSLOT)]
        s_xin = S.new_sem("s_xin")
        s_xout = S.new_sem("s_xout")
        s_par = S.new_sem("s_par")
        s_cc = S.new_sem("s_cc")
        s_st = S.new_sem("s_st")
        s_rcv = S.new_sem("s_rcv")
        s_och = [S.new_sem(f"s_och{i}") for i in range(NCH)]
        s_sth = [S.new_sem(f"s_sth{i}") for i in range(8)]
        s_scr = [S.new_sem(f"s_scr{i}") for i in range(NSLOT)]
        s_dbg = S.new_sem("s_dbg")
        dbg_i = [0]

        def dump_xres():
            if not debug or dbg_i[0] >= 8:
                return
            k = dbg_i[0]
            dbg_i[0] += 1
            S.op("sp", [lambda e, k=k: e.dma_start(out=dbg[k].rearrange("(c p) t -> p c t", p=128), in_=xres[:])],
                 reads=XRES_, dma=s_dbg)

        tpool = Pool_(NTMP)
        tbpool = Pool_(NTB)
        slots = Pool_(NSLOT)
        bank_ctr = [0]

        reserved = set()

        def next_bank():
            while True:
                b = bank_ctr[0] % 8
                bank_ctr[0] += 1
                if b not in reserved:
                    return b

        def reserve_bank():
            b = next_bank()
            reserved.add(b)
            return b

        def release_bank(b):
            reserved.discard(b)

        def P(name, l, c, n=1):
            o = _PAR[(name, l)] + c
            return par[:, o:o + n]

        def Dv(name, l, c, n=1):
            o = _DER[(name, l)] + c
            return der[:, o:o + n]

        XRES = XRES_
        HNA = [("hnA", c) for c in range(NCH)]
        HNB = [("hnB", c) for c in range(NCH)]

        S.op("sp", [lambda e: e.dma_start(out=par[:], in_=par_d)], writes=["par"], dma=s_par)
        S.op("dve", [lambda e: e.memset(car_in[0][:], 0.0)], writes=[("carin", 0)])
        S.op("dve", [lambda e: e.memset(xres[:], 0.0)], writes=XRES_)
        S.op("dve", [lambda e: e.memset(ones[:], 1.0 / D)], writes=["ones"])
        for l in range(DEPTH):
            S.op("act", [lambda e, l=l: e.activation(out=Dv("e", l, 0, 16), in_=P("lam", l, 0, 16),
                                                     func=AF.Exp, scale=-1.0)],
                 reads=["par"], writes=[("der_e", l)])
            S.op("act", [lambda e, l=l: e.activation(out=Dv("e", l, 0, 16), in_=Dv("e", l, 0, 16),
                                                     func=AF.Ln, bias=1.0)],
                 reads=[("der_e", l)], writes=[("der_e", l)])
            S.op("act", [lambda e, l=l: e.activation(out=Dv("hls", l, 0, 16), in_=Dv("e", l, 0, 16),
                                                     func=AF.Identity, scale=-4.0)],
                 reads=[("der_e", l)], writes=["der"])
            S.op("act", [lambda e, l=l: e.activation(out=Dv("hbr", l, 0, 16), in_=P("br", l, 0, 16),
                                                     func=AF.Identity, scale=0.5)],
                 reads=["par"], writes=["der"])
            S.op("act", [lambda e, l=l: e.activation(out=Dv("hbi", l, 0, 16), in_=P("bi", l, 0, 16),
                                                     func=AF.Identity, scale=0.5)],
                 reads=["par"], writes=["der"])
            S.op("act", [lambda e, l=l: e.activation(out=Dv("hbg", l, 0, 32), in_=P("bin", l, 80, 32),
                                                     func=AF.Identity, scale=0.5)],
                 reads=["par"], writes=["der"])
        PR = ["par", "der"]

        def load_panel(src_aps):
            s = slots.get()
            for lo, hi, ap in src_aps:
                S.op("pool", [lambda e, s=s, lo=lo, hi=hi, ap=ap: e.dma_start(out=wsl[s][:, :, lo:hi], in_=ap)],
                     writes=[("w", s)], dma=wsem[s])
            return s

        def win_panel(l, col0):
            return load_panel([(0, 512, w_in[l].rearrange("(kc p) n -> p kc n", p=128)[:, :, col0:col0 + 512])])

        def sq_panel(w, l, col0):
            return load_panel([(0, 512, w[l].rearrange("(kc p) n -> p kc n", p=128)[:, :, col0:col0 + 512])])

        def cached_panel(scr, name, l, idx, sl, src_ap):
            key = ("scr", name, l, idx)
            if sl < 2:
                s_ = load_panel([(0, 512, src_ap)])
                S.op("sp", [lambda e, s_=s_: e.dma_start(out=scr[l, idx], in_=wsl[s_][:].rearrange("p a b -> p (a b)"))],
                     reads=[("w", s_)], writes=[key], dma=s_scr[s_])
                return s_
            s_ = slots.get()
            S.op("pool", [lambda e, s_=s_: e.dma_start(out=wsl[s_][:].rearrange("p a b -> p (a b)"), in_=scr[l, idx])],
                 reads=[key], writes=[("w", s_)], dma=wsem[s_])
            return s_

        def mm_group(bank, pairs, reads):
            n = len(pairs)
            fns = []
            for i, (lt, rh) in enumerate(pairs):
                fns.append(lambda e, lt=lt, rh=rh, i=i: e.matmul(ps[:, bank, :], lhsT=lt, rhs=rh,
                                                               start=(i == 0), stop=(i == n - 1)))
            S.op("pe", fns, reads=reads, writes=[("ps", bank)])

        def proj_group(slot, cc, src, src_keys, nk=16, k0=0):
            bank = next_bank()
            pairs = [(wsl[slot][:, kc, cc * 128:(cc + 1) * 128], src[:, k0 + kc, :]) for kc in range(nk)]
            mm_group(bank, pairs, reads=[("w", slot)] + src_keys)
            return bank

        def norm(gname, gl, dst, dst_keys, inplace=False):
            bank = next_bank()
            for c in range(NCH):
                k = tbpool.get()
                S.op("act", [lambda e, c=c, k=k: e.activation(out=tb[k][:, :], in_=xres[:, c, :], func=AF.Square)],
                     reads=[("xres", c)], writes=[("tb", k)])
                S.op("pe", [lambda e, c=c, k=k: e.matmul(ps[:, bank, :], lhsT=ones[:, :], rhs=tb[k][:, :],
                                                         start=(c == 0), stop=(c == NCH - 1))],
                     reads=[("tb", k), "ones"], writes=[("ps", bank)])
                tbpool.put(k)
            r = tpool.get()
            S.op("act", [lambda e: e.activation(out=tmp[r][:, 0:T], in_=ps[:, bank, :], func=AF.Sqrt, bias=EPS)],
                 reads=[("ps", bank)], writes=[("tmp", r)])
            S.op("dve", [lambda e: e.reciprocal(out=tmp[r][:, 0:T], in_=tmp[r][:, 0:T])],
                 reads=[("tmp", r)], writes=[("tmp", r)])
            for c in range(NCH):
                o = xres[:, c, :] if inplace else dst[:, c, :]
                S.op("dve", [lambda e, c=c, o=o: e.scalar_tensor_tensor(
                    out=o, in0=xres[:, c, :], scalar=P(gname, gl, c), in1=tmp[r][:, 0:T],
                    op0=ALU.mult, op1=ALU.mult)],
                     reads=[("xres", c), ("tmp", r)] + PR, writes=[dst_keys[c]])
            tpool.put(r)

        xT_v = xT.rearrange("(c p) t -> p c t", p=128)
        stages = {}

        def stage_ap(st, c):
            return hnBst[:, c, :] if c < 8 else tmp[st["tm"][c - 8]][:, 0:T]

        def stage_keys(st, c):
            return [("hnB", 2 * c), ("hnB", 2 * c + 1)] if c < 8 else [("tmp", st["tm"][c - 8])]

        def load_stage_lo(sl):
            stages[sl] = {}
            S.op("sp", [lambda e, sl=sl: e.dma_start(out=hnBst[:, :, :], in_=xT_v[:, 0:8, sl * T:(sl + 1) * T])],
                 writes=HNB, dma=s_xin)

        def load_stage_hi(sl):
            st = stages[sl]
            st["tm"] = [tpool.get() for _ in range(8)]
            for k in range(8):
                S.op("sp", [lambda e, sl=sl, k=k, t_=st["tm"][k]: e.dma_start(
                    out=tmp[t_][:, 0:T], in_=xT_v[:, 8 + k, sl * T:(sl + 1) * T])],
                     writes=[("tmp", st["tm"][k])], dma=s_sth[k])

        def stage_stat(sl):
            st = stages[sl]
            bank = reserve_bank()
            for c in range(NCH):
                k = tbpool.get()
                sa, sk = stage_ap(st, c), stage_keys(st, c)
                S.op("act", [lambda e, sa=sa, k=k: e.activation(out=tb[k][:, :], in_=sa, func=AF.Square)],
                     reads=sk, writes=[("tb", k)])
                S.op("pe", [lambda e, c=c, k=k, bank=bank: e.matmul(ps[:, bank, :], lhsT=ones[:, :], rhs=tb[k][:, :],
                                                                    start=(c == 0), stop=(c == NCH - 1))],
                     reads=[("tb", k), "ones"], writes=[("ps", bank)])
                tbpool.put(k)
            st["mst"] = tpool.get()
            S.op("act", [lambda e, bank=bank, m=st["mst"]: e.activation(out=tmp[m][:, 0:T], in_=ps[:, bank, :],
                                                                      func=AF.Identity)],
                 reads=[("ps", bank)], writes=[("tmp", st["mst"])])
            release_bank(bank)

        def layer(l, sl):
            ci, co = car_in[sl % 2], car_out[sl % 2]
            CI, CO = ("carin", sl % 2), ("carout", sl % 2)

            G = load_panel([
                (0, 256, lru_wr[l].rearrange("n (hc p) k -> p (n hc) k", p=128)),
                (256, 512, lru_wi[l].rearrange("n (hc p) k -> p (n hc) k", p=128)),
            ])
            for jp in range(4):
                sx = win_panel(l, 0 * D + jp * 512)
                xa_t = {}
                xab_t = {}
                for cc in range(4):
                    c = jp * 4 + cc
                    bank = proj_group(sx, cc, hnA, HNA)
                    xr = tpool.get()
                    S.op("act", [lambda e, c=c, xr=xr, bank=bank: e.activation(
                        out=tmp[xr][:, 3:3 + T], in_=ps[:, bank, :], func=AF.Identity, bias=P("bin", l, c))],
                         reads=[("ps", bank)] + PR, writes=[("tmp", xr)])
                    S.op("dve", [lambda e, c=c, xr=xr: e.tensor_copy(out=tmp[xr][:, 0:3], in_=ci[:, 16 + 3 * c:19 + 3 * c])],
                         reads=[CI, ("tmp", xr)], writes=[("tmp", xr)])
                    xa = tpool.get()
                    S.op("dve", [lambda e, c=c, xr=xr, xa=xa: e.tensor_scalar(
                        out=tmp[xa][:, 0:T], in0=tmp[xr][:, 3:3 + T], scalar1=P("caw", l, 3 * 16 + c),
                        scalar2=P("cab", l, c), op0=ALU.mult, op1=ALU.add)],
                         reads=[("tmp", xr)] + PR, writes=[("tmp", xa)])
                    for k in (2, 1, 0):
                        S.op("dve", [lambda e, c=c, xr=xr, xa=xa, k=k: e.scalar_tensor_tensor(
                            out=tmp[xa][:, 0:T], in0=tmp[xr][:, k:k + T], scalar=P("caw", l, k * 16 + c),
                            in1=tmp[xa][:, 0:T], op0=ALU.mult, op1=ALU.add)],
                             reads=[("tmp", xr), ("tmp", xa)] + PR, writes=[("tmp", xa)])
                    S.op("dve", [lambda e, c=c, xr=xr: e.tensor_copy(out=co[:, 16 + 3 * c:19 + 3 * c], in_=tmp[xr][:, T:T + 3])],
                         reads=[("tmp", xr)], writes=[CO])
                    tpool.put(xr)
                    kb = tbpool.get()
                    S.op("dve", [lambda e, xa=xa, kb=kb: e.tensor_copy(out=tb[kb][:, :], in_=tmp[xa][:, 0:T])],
                         reads=[("tmp", xa)], writes=[("tb", kb)])
                    xa_t[c] = xa
                    xab_t[c] = kb
                slots.put(sx)
                sy = win_panel(l, 1 * D + jp * 512)
                gel_t = {}
                for cc in range(4):
                    c = jp * 4 + cc
                    bank = proj_group(sy, cc, hnA, HNA)
                    g = tpool.get()
                    S.op("act", [lambda e, c=c, g=g, bank=bank: e.activation(
                        out=tmp[g][:, 0:T], in_=ps[:, bank, :], func=AF.Gelu_apprx_tanh, bias=P("bin", l, 16 + c))],
                         reads=[("ps", bank)] + PR, writes=[("tmp", g)])
                    gel_t[c] = g
                slots.put(sy)
                for n in (2 * jp, 2 * jp + 1):
                    cs = (2 * n, 2 * n + 1)
                    banks = {}
                    for which, off in (("r", 0), ("i", 256)):
                        for kc in range(2):
                            bank = next_bank()
                            pairs = [(wsl[G][:, n * 2 + hc, off + kc * 128: off + (kc + 1) * 128], tb[xab_t[cs[hc]]][:, :])
                                     for hc in range(2)]
                            mm_group(bank, pairs, reads=[("w", G), ("tb", xab_t[cs[0]]), ("tb", xab_t[cs[1]])])
                            banks[(which, kc)] = bank
                    for kc in range(2):
                        c = cs[kc]
                        R = tpool.get()
                        I = tpool.get()
                        M = tpool.get()
                        br_, bi_ = banks[("r", kc)], banks[("i", kc)]
                        xa = xa_t[c]
                        g = gel_t[c]
                        S.op("act", [lambda e, c=c, R=R, br_=br_: e.activation(
                            out=tmp[R][:, 0:T], in_=ps[:, br_, :], func=AF.Tanh, scale=0.5, bias=Dv("hbr", l, c))],
                             reads=[("ps", br_)] + PR, writes=[("tmp", R)])
                        S.op("act", [lambda e, c=c, I=I, bi_=bi_: e.activation(
                            out=tmp[I][:, 0:T], in_=ps[:, bi_, :], func=AF.Tanh, scale=0.5, bias=Dv("hbi", l, c))],
                             reads=[("ps", bi_)] + PR, writes=[("tmp", I)])
                        S.op("act", [lambda e, c=c, R=R: e.activation(
                            out=tmp[R][:, 0:T], in_=tmp[R][:, 0:T], func=AF.Exp, scale=Dv("hls", l, c),
                            bias=Dv("hls", l, c))],
                             reads=[("tmp", R)] + PR, writes=[("tmp", R)])
                        S.op("act", [lambda e, R=R, M=M: e.activation(
                            out=tmp[M][:, 0:T], in_=tmp[R][:, 0:T], func=AF.Square)],
                             reads=[("tmp", R)], writes=[("tmp", M)])
                        S.op("act", [lambda e, M=M: e.activation(
                            out=tmp[M][:, 0:T], in_=tmp[M][:, 0:T], func=AF.Sqrt, scale=-0.25, bias=0.25)],
                             reads=[("tmp", M)], writes=[("tmp", M)])
                        S.op("dve", [lambda e, I=I, xa=xa: e.scalar_tensor_tensor(
                            out=tmp[I][:, 0:T], in0=tmp[I][:, 0:T], scalar=1.0, in1=tmp[xa][:, 0:T],
                            op0=ALU.add, op1=ALU.mult)],
                             reads=[("tmp", I), ("tmp", xa)], writes=[("tmp", I)])
                        S.op("dve", [lambda e, I=I, M=M: e.tensor_tensor(
                            out=tmp[I][:, 0:T], in0=tmp[I][:, 0:T], in1=tmp[M][:, 0:T], op=ALU.mult)],
                             reads=[("tmp", I), ("tmp", M)], writes=[("tmp", I)])
                        S.op("dve", [lambda e, c=c, R=R, I=I, M=M: e.tensor_tensor_scan(
                            out=tmp[M][:, 0:T], data0=tmp[R][:, 0:T], data1=tmp[I][:, 0:T],
                            initial=ci[:, c:c + 1], op0=ALU.mult, op1=ALU.add)],
                             reads=[("tmp", R), ("tmp", I), CI, ("tmp", M)], writes=[("tmp", M)])
                        S.op("dve", [lambda e, c=c, M=M: e.tensor_copy(out=co[:, c:c + 1], in_=tmp[M][:, T - 1:T])],
                             reads=[("tmp", M)], writes=[CO])
                        S.op("dve", [lambda e, c=c, M=M, g=g: e.tensor_tensor(
                            out=U[:, c, :], in0=tmp[M][:, 0:T], in1=tmp[g][:, 0:T], op=ALU.mult)],
                             reads=[("tmp", M), ("tmp", g)], writes=[("U", c)])
                        for t_ in (R, I, M, xa, g):
                            tpool.put(t_)
                        tbpool.put(xab_t[c])
            slots.put(G)

            for jp in range(4):
                sc = win_panel(l, 3 * D + jp * 512)
                ccs = {}
                for cc in range(4):
                    c = jp * 4 + cc
                    bank = proj_group(sc, cc, hnA, HNA)
                    t1 = tpool.get()
                    S.op("act", [lambda e, c=c, t1=t1, bank=bank: e.activation(
                        out=tmp[t1][:, 0:T], in_=ps[:, bank, :], func=AF.Identity, bias=P("bin", l, 48 + c))],
                         reads=[("ps", bank)] + PR, writes=[("tmp", t1)])
                    ccs[c] = t1
                slots.put(sc)
                sxx = win_panel(l, 4 * D + jp * 512)
                for cc in range(4):
                    c = jp * 4 + cc
                    bank = proj_group(sxx, cc, hnA, HNA)
                    t1 = ccs[c]
                    p = tpool.get()
                    S.op("dve", [lambda e, c=c, t1=t1, p=p, bank=bank: e.scalar_tensor_tensor(
                        out=tmp[p][:, 2:2 + T], in0=ps[:, bank, :], scalar=P("bin", l, 64 + c), in1=tmp[t1][:, 0:T],
                        op0=ALU.add, op1=ALU.mult)],
                         reads=[("ps", bank), ("tmp", t1)] + PR, writes=[("tmp", p)])
                    S.op("dve", [lambda e, c=c, p=p: e.tensor_copy(out=tmp[p][:, 0:2], in_=ci[:, 64 + 2 * c:66 + 2 * c])],
                         reads=[CI, ("tmp", p)], writes=[("tmp", p)])
                    S.op("dve", [lambda e, c=c, t1=t1, p=p: e.tensor_scalar(
                        out=tmp[t1][:, 0:T], in0=tmp[p][:, 2:2 + T], scalar1=P("cbw", l, 2 * 16 + c), scalar2=None,
                        op0=ALU.mult)],
                         reads=[("tmp", p), ("tmp", t1)] + PR, writes=[("tmp", t1)])
                    for k in (1, 0):
                        S.op("dve", [lambda e, c=c, t1=t1, p=p, k=k: e.scalar_tensor_tensor(
                            out=tmp[t1][:, 0:T], in0=tmp[p][:, k:k + T], scalar=P("cbw", l, k * 16 + c),
                            in1=tmp[t1][:, 0:T], op0=ALU.mult, op1=ALU.add)],
                             reads=[("tmp", p), ("tmp", t1)] + PR, writes=[("tmp", t1)])
                    S.op("dve", [lambda e, c=c, p=p: e.tensor_copy(out=co[:, 64 + 2 * c:66 + 2 * c], in_=tmp[p][:, T:T + 2])],
                         reads=[("tmp", p)], writes=[CO])
                    tpool.put(p)
                slots.put(sxx)
                sb_ = win_panel(l, 2 * D + jp * 512)
                for cc in range(4):
                    c = jp * 4 + cc
                    bank = proj_group(sb_, cc, hnA, HNA)
                    t1 = ccs[c]
                    S.op("dve", [lambda e, c=c, t1=t1, bank=bank: e.scalar_tensor_tensor(
                        out=U[:, 16 + c, :], in0=ps[:, bank, :], scalar=P("bin", l, 32 + c), in1=tmp[t1][:, 0:T],
                        op0=ALU.add, op1=ALU.mult)],
                         reads=[("ps", bank), ("tmp", t1)] + PR, writes=[("U", 16 + c)])
                    tpool.put(t1)
                slots.put(sb_)

            UA = [("U", c) for c in range(16)]
            UB = [("U", 16 + c) for c in range(16)]
            for jp in range(4):
                tg = {}
                for which, seg, wmat, src_k0, src_keys, hoff in (("a", 5, w_pa, 0, UA, 0), ("b", 6, w_pb, 16, UB, 16)):
                    sg = win_panel(l, seg * D + jp * 512)
                    for cc in range(4):
                        c = jp * 4 + cc
                        bank = proj_group(sg, cc, hnA, HNA)
                        t1 = tpool.get()
                        S.op("act", [lambda e, c=c, t1=t1, bank=bank, hoff=hoff: e.activation(
                            out=tmp[t1][:, 0:T], in_=ps[:, bank, :], func=AF.Tanh, scale=0.5,
                            bias=Dv("hbg", l, hoff + c))],
                             reads=[("ps", bank)] + PR, writes=[("tmp", t1)])
                        tg[(which, c)] = t1
                    slots.put(sg)
                    spj = sq_panel(wmat, l, jp * 512)
                    for cc in range(4):
                        c = jp * 4 + cc
                        bank = proj_group(spj, cc, U, src_keys, k0=src_k0)
                        t1 = tg[(which, c)]
                        S.op("dve", [lambda e, t1=t1, bank=bank: e.scalar_tensor_tensor(
                            out=tmp[t1][:, 0:T], in0=tmp[t1][:, 0:T], scalar=1.0, in1=ps[:, bank, :],
                            op0=ALU.add, op1=ALU.mult)],
                             reads=[("ps", bank), ("tmp", t1)], writes=[("tmp", t1)])
                    slots.put(spj)
                if jp == 0 and sl < ns - 1:
                    S.op("sp", [lambda e: e.dma_start(out=cin_d[sl], in_=co[:])], reads=[CO], writes=[("cin_d", sl)],
                         dma=s_st)
                    S.op("pool", [lambda e: e.collective_compute(
                        "AllGather", ALU.bypass, replica_groups=[[2 * i, 2 * i + 1] for i in range(n_pairs)],
                        ins=[cin_d[sl].opt()], outs=[cout_d[sl].opt()])],
                         reads=[("cin_d", sl)], writes=[("cout_d", sl)], sem=s_cc)
                for cc in range(4):
                    c = jp * 4 + cc
                    ta, tb_ = tg[("a", c)], tg[("b", c)]
                    S.op("dve", [lambda e, c=c, ta=ta, tb_=tb_: e.tensor_tensor(
                        out=hnB[:, c, :], in0=tmp[ta][:, 0:T], in1=tmp[tb_][:, 0:T], op=ALU.add)],
                         reads=[("tmp", ta), ("tmp", tb_)], writes=[("hnB", c)])
                    tpool.put(ta)
                    tpool.put(tb_)

            sbank = reserve_bank()
            pend = []

            def stat_chunk(c):
                k = tbpool.get()
                S.op("act", [lambda e, c=c, k=k: e.activation(out=tb[k][:, :], in_=xres[:, c, :], func=AF.Square)],
                     reads=[("xres", c)], writes=[("tb", k)])
                S.op("pe", [lambda e, c=c, k=k: e.matmul(ps[:, sbank, :], lhsT=ones[:, :], rhs=tb[k][:, :],
                                                         start=(c == 0), stop=(c == NCH - 1))],
                     reads=[("tb", k), "ones"], writes=[("ps", sbank)])
                tbpool.put(k)

            for jp in range(4):
                so = sq_panel(w_o, l, jp * 512)
                for cc in range(4):
                    c = jp * 4 + cc
                    bank = proj_group(so, cc, hnB, HNB)
                    S.op("dve", [lambda e, c=c, bank=bank: e.scalar_tensor_tensor(
                        out=xres[:, c, :], in0=ps[:, bank, :], scalar=0.5, in1=xres[:, c, :],
                        op0=ALU.mult, op1=ALU.add)],
                         reads=[("ps", bank), ("xres", c)], writes=[("xres", c)])
                    S.op("dve", [lambda e, c=c: e.tensor_scalar(
                        out=hnA[:, c, :], in0=xres[:, c, :], scalar1=P("g2", l, c), scalar2=None, op0=ALU.mult)],
                         reads=[("xres", c)] + PR, writes=[("hnA", c)])
                    pend.append(c)
                    if len(pend) > 3:
                        stat_chunk(pend.pop(0))
                slots.put(so)
            if sl + 1 < ns:
                load_stage_lo(sl + 1)

            dump_xres()
            r2 = None
            for half in range(2):
                for jp in range(8):
                    pi = half * 8 + jp
                    s1 = cached_panel(scr1, "m1", l, pi, sl, w_mlp1[l].rearrange("(kc p) n -> p kc n", p=128)
                                      [:, :, pi * 512:(pi + 1) * 512])
                    for cc in range(4):
                        uc = jp * 4 + cc
                        bank = proj_group(s1, cc, hnA, HNA)
                        t1 = tpool.get()
                        S.op("act", [lambda e, t1=t1, bank=bank: e.activation(
                            out=tmp[t1][:, 0:T], in_=ps[:, bank, :], func=AF.Relu)],
                             reads=[("ps", bank)], writes=[("tmp", t1)])
                        S.op("dve", [lambda e, uc=uc, t1=t1, bank=bank: e.tensor_tensor(
                            out=U[:, uc, :], in0=tmp[t1][:, 0:T], in1=ps[:, bank, :], op=ALU.mult)],
                             reads=[("ps", bank), ("tmp", t1)], writes=[("U", uc)])
                        tpool.put(t1)
                        if pend:
                            stat_chunk(pend.pop(0))
                    slots.put(s1)
                    if half == 0 and jp == 1:
                        r2 = tpool.get()
                        S.op("dve", [lambda e, r2=r2: e.tensor_scalar(
                            out=tmp[r2][:, 0:T], in0=ps[:, sbank, :], scalar1=EPS, scalar2=None, op0=ALU.add)],
                             reads=[("ps", sbank)], writes=[("tmp", r2)])
                        S.op("dve", [lambda e, r2=r2: e.reciprocal(out=tmp[r2][:, 0:T], in_=tmp[r2][:, 0:T])],
                             reads=[("tmp", r2)], writes=[("tmp", r2)])
                        release_bank(sbank)
                if half == 1 and sl + 1 < ns:
                    load_stage_hi(sl + 1)
                for j in range(4):
                    banks = [next_bank() for _ in range(4)]
                    for g in range(2):
                        gi = half * 2 + g
                        s2 = cached_panel(scr2, "m2", l, gi * 4 + j, sl,
                                          w_mlp2[l].rearrange("(g kc p) n -> p g kc n", p=128, kc=16)
                                          [:, gi, :, j * 512:(j + 1) * 512])
                        for m in range(4):
                            fns = []
                            for kc in range(16):
                                fns.append(lambda e, m=m, kc=kc, g=g, s2=s2, bk=banks[m]: e.matmul(
                                    ps[:, bk, :], lhsT=wsl[s2][:, kc, m * 128:(m + 1) * 128],
                                    rhs=U[:, g * 16 + kc, :], start=(g == 0 and kc == 0), stop=(g == 1 and kc == 15)))
                            S.op("pe", fns, reads=[("w", s2)] + [("U", g * 16 + kc) for kc in range(16)],
                                 writes=[("ps", banks[m])])
                        slots.put(s2)
                    for m in range(4):
                        c = j * 4 + m
                        t1 = tpool.get()
                        S.op("dve", [lambda e, t1=t1, r2=r2, bank=banks[m]: e.tensor_tensor(
                            out=tmp[t1][:, 0:T], in0=ps[:, bank, :], in1=tmp[r2][:, 0:T], op=ALU.mult)],
                             reads=[("ps", banks[m]), ("tmp", r2)], writes=[("tmp", t1)])
                        S.op("dve", [lambda e, c=c, t1=t1: e.tensor_tensor(
                            out=xres[:, c, :], in0=tmp[t1][:, 0:T], in1=xres[:, c, :], op=ALU.add)],
                             reads=[("tmp", t1), ("xres", c)], writes=[("xres", c)])
                        tpool.put(t1)
                    if half == 1 and j == 1 and sl + 1 < ns:
                        stage_stat(sl + 1)
            tpool.put(r2)

        oT_v = outT.rearrange("(c p) t -> p c t", p=128)
        out_toks = []

        def final_stat():
            bank = reserve_bank()
            for c in range(NCH):
                k = tbpool.get()
                S.op("act", [lambda e, c=c, k=k: e.activation(out=tb[k][:, :], in_=xres[:, c, :], func=AF.Square)],
                     reads=[("xres", c)], writes=[("tb", k)])
                S.op("pe", [lambda e, c=c, k=k, bank=bank: e.matmul(ps[:, bank, :], lhsT=ones[:, :], rhs=tb[k][:, :],
                                                                    start=(c == 0), stop=(c == NCH - 1))],
                     reads=[("tb", k), "ones"], writes=[("ps", bank)])
                tbpool.put(k)
            return bank

        def rstd_from(src_ap, src_keys):
            r = tpool.get()
            S.op("act", [lambda e, r=r: e.activation(out=tmp[r][:, 0:T], in_=src_ap, func=AF.Sqrt, bias=EPS)],
                 reads=src_keys, writes=[("tmp", r)])
            S.op("dve", [lambda e, r=r: e.reciprocal(out=tmp[r][:, 0:T], in_=tmp[r][:, 0:T])],
                 reads=[("tmp", r)], writes=[("tmp", r)])
            return r

        def emit_outputs(sl_prev, rf):
            for c in range(NCH):
                o = tpool.get()
                S.op("dve", [lambda e, c=c, o=o, rf=rf: e.scalar_tensor_tensor(
                    out=tmp[o][:, 0:T], in0=xres[:, c, :], scalar=P("gf", 0, c), in1=tmp[rf][:, 0:T],
                    op0=ALU.mult, op1=ALU.mult)],
                     reads=[("xres", c), ("tmp", rf)] + PR, writes=[("tmp", o)])
                tok = S.op("sp", [lambda e, c=c, o=o: e.dma_start(
                    out=oT_v[:, c, sl_prev * T:(sl_prev + 1) * T], in_=tmp[o][:, 0:T])],
                           reads=[("tmp", o)], dma=s_och[c])
                out_toks.append(tok)
                tpool.put(o)

        def boundary(sl, st):
            l = sl % 2
            bank = final_stat()
            m1 = tpool.get()
            S.op("dve", [lambda e: e.scalar_tensor_tensor(
                out=tmp[m1][:, 0:T], in0=ps[:, bank, :], scalar=P("keep", 0, sl), in1=tmp[st["mst"]][:, 0:T],
                op0=ALU.mult, op1=ALU.add)],
                 reads=[("ps", bank), ("tmp", st["mst"])] + PR, writes=[("tmp", m1)])
            S.op("act", [lambda e: e.activation(out=tmp[m1][:, 0:T], in_=tmp[m1][:, 0:T], func=AF.Sqrt, bias=EPS)],
                 reads=[("tmp", m1)], writes=[("tmp", m1)])
            S.op("dve", [lambda e: e.reciprocal(out=tmp[m1][:, 0:T], in_=tmp[m1][:, 0:T])],
                 reads=[("tmp", m1)], writes=[("tmp", m1)])
            rf = None
            if sl >= 1:
                rf = tpool.get()
                S.op("act", [lambda e: e.activation(out=tmp[rf][:, 0:T], in_=ps[:, bank, :], func=AF.Sqrt, bias=EPS)],
                     reads=[("ps", bank)], writes=[("tmp", rf)])
            release_bank(bank)
            for c in range(NCH):
                sa, sk = stage_ap(st, c), stage_keys(st, c)
                S.op("dve", [lambda e, c=c, sa=sa: e.scalar_tensor_tensor(
                    out=sa, in0=xres[:, c, :], scalar=P("keep", 0, sl), in1=sa, op0=ALU.mult, op1=ALU.add)],
                     reads=[("xres", c)] + sk + PR, writes=sk)
                S.op("dve", [lambda e, c=c, sa=sa: e.scalar_tensor_tensor(
                    out=hnA[:, c, :], in0=sa, scalar=P("g1", l, c), in1=tmp[m1][:, 0:T], op0=ALU.mult, op1=ALU.mult)],
                     reads=sk + [("tmp", m1)] + PR, writes=[("hnA", c)])
            if sl >= 1:
                S.op("dve", [lambda e: e.reciprocal(out=tmp[rf][:, 0:T], in_=tmp[rf][:, 0:T])],
                     reads=[("tmp", rf)], writes=[("tmp", rf)])
                emit_outputs(sl - 1, rf)
                tpool.put(rf)
            for c in range(NCH):
                sa, sk = stage_ap(st, c), stage_keys(st, c)
                S.op("act", [lambda e, c=c, sa=sa: e.activation(out=xres[:, c, :], in_=sa, func=AF.Identity)],
                     reads=sk, writes=[("xres", c)])
            tpool.put(m1)
            tpool.put(st["mst"])
            for t_ in st["tm"]:
                tpool.put(t_)

        load_stage_lo(0)
        load_stage_hi(0)
        stage_stat(0)
        for sl in range(ns):
            l = sl % 2
            if sl >= 1:
                for r_ in range(2):
                    S.op("sp", [lambda e, sl=sl, r_=r_: e.dma_start(out=rcv[:, r_, :],
                                                                  in_=cout_d[sl - 1][r_ * 128:(r_ + 1) * 128, :])],
                         reads=[("cout_d", sl - 1)], writes=["rcv"], dma=s_rcv)
                ci = car_in[sl % 2]
                S.op("dve", [lambda e, sl=sl, ci=ci: e.tensor_scalar(
                    out=ci[:], in0=rcv[:, 0, :], scalar1=P("m0", 0, sl), scalar2=None, op0=ALU.mult)],
                     reads=["rcv"] + PR, writes=[("carin", sl % 2)])
                S.op("dve", [lambda e, sl=sl, ci=ci: e.scalar_tensor_tensor(
                    out=ci[:], in0=rcv[:, 1, :], scalar=P("m1", 0, sl), in1=ci[:], op0=ALU.mult, op1=ALU.add)],
                     reads=["rcv", ("carin", sl % 2)] + PR, writes=[("carin", sl % 2)])
            boundary(sl, stages.pop(sl))
            layer(l, sl)
            if debug:
                dump_xres()
        bank = final_stat()
        rf = rstd_from(ps[:, bank, :], [("ps", bank)])
        release_bank(bank)
        emit_outputs(ns - 1, rf)
        tpool.put(rf)
        last = {}
        for sname, v in out_toks:
            last[sname] = max(last.get(sname, 0), v)
        S.final_wait("sp", list(last.items()))

        with nc.Block() as block:
            @block.sync
            def _(e):
                S.replay("sp", e)

            @block.tensor
            def _(e):
                S.replay("pe", e)

            @block.scalar
            def _(e):
                S.replay("act", e)

            @block.vector
            def _(e):
                S.replay("dve", e)

            @block.gpsimd
            def _(e):
                S.replay("pool", e)
    return nc


def _chunked(v, n):
    return np.ascontiguousarray(np.asarray(v, np.float32).reshape(n, 128).T)


def pack_params(inp):
    par = np.zeros((128, NPAR), np.float32)

    def put(name, l, arr):
        o = _PAR[(name, l)]
        par[:, o:o + arr.shape[1]] = arr

    for l in range(DEPTH):
        put("g1", l, _chunked(inp["norm1_g"][l], 16))
        put("bin", l, _chunked(inp["b_in"][l], 112))
        put("caw", l, np.concatenate([_chunked(inp["conv_a_w"][l][k], 16) for k in range(4)], axis=1))
        put("cab", l, _chunked(inp["conv_a_b"][l], 16))
        put("br", l, _chunked(inp["lru_br"][l].reshape(-1), 16))
        put("bi", l, _chunked(inp["lru_bi"][l].reshape(-1), 16))
        put("lam", l, _chunked(inp["lru_lam"][l], 16))
        put("cbw", l, np.concatenate([_chunked(inp["conv_b_w"][l][k], 16) for k in range(3)], axis=1))
        put("g2", l, _chunked(inp["norm2_g"][l], 16))
    put("gf", 0, _chunked(inp["final_g"], 16))
    return par


N_CORES = 8
_CACHE = {}
_WNAMES = ["w_in", "lru_wr", "lru_wi", "w_pa", "w_pb", "w_o", "w_mlp1", "w_mlp2"]
_PNAMES = ["norm1_g", "b_in", "conv_a_w", "conv_a_b", "lru_br", "lru_bi", "lru_lam", "conv_b_w", "norm2_g"]


def core_inputs(inputs, b, r, ws, ws_sw):
    x = inputs["x"]
    pin = dict(inputs)
    if r == 1:
        for k in _PNAMES:
            pin[k] = np.asarray(inputs[k])[::-1]
    par = pack_params(pin)
    for sl in range(NS):
        fresh = 1.0 if sl % 2 == r else 0.0
        par[:, _PAR[("fr", 0)] + sl] = fresh
        par[:, _PAR[("keep", 0)] + sl] = 1.0 - fresh
        if r == 0:
            par[:, _PAR[("m1", 0)] + sl] = 1.0 if sl >= 2 else 0.0
        else:
            par[:, _PAR[("m0", 0)] + sl] = 1.0 if sl >= 1 else 0.0
    xT = np.zeros((D, NS * T), np.float32)
    for sl in range(NS):
        if sl % 2 == r and sl < 8:
            xT[:, sl * T:(sl + 1) * T] = x[b, sl * T:(sl + 1) * T, :].T
    m = {"xT": xT, "par": par}
    m.update(ws_sw if r == 1 else ws)
    return m


def kernel(**inputs):
    inputs = {k: np.asarray(v, np.float32) for k, v in inputs.items()}
    x = inputs["x"]
    B, SEQ, _ = x.shape
    assert B * 2 == N_CORES and SEQ == 8 * T
    if "nc" not in _CACHE:
        _CACHE["nc"] = build_program()
    nc = _CACHE["nc"]
    ws = {k: np.ascontiguousarray(inputs[k]) for k in _WNAMES}
    ws_sw = {k: np.ascontiguousarray(inputs[k][::-1]) for k in _WNAMES}
    in_maps = [core_inputs(inputs, c // 2, c % 2, ws, ws_sw) for c in range(N_CORES)]
    res = run_bass_kernel_spmd(nc, in_maps, core_ids=list(range(N_CORES)))
    out = np.empty((B, SEQ, D), np.float32)
    for b in range(B):
        for g in range(8):
            o = res.results[2 * b + g % 2]["outT"]
            out[b, g * T:(g + 1) * T, :] = o[:, (g + 1) * T:(g + 2) * T].T
    return out
```

```python
import contextlib
import numpy as np
import concourse.bass as bass
import concourse.mybir as mybir
from concourse.bass_utils import run_bass_kernel_spmd

F32 = mybir.dt.float32
BF16 = mybir.dt.bfloat16
AF = mybir.ActivationFunctionType
ALU = mybir.AluOpType

D = 2048
NCH = 16
T = 512
DEPTH = 2
N_IN = 14336
D_FF = 8192
EPS = 1e-6
NSLOT = 4
CACHE_MLP_BF16 = False
NTMP = 18
NTB = 5
TMPW = 516

_PAR = {}
_off = 0
for _l in range(DEPTH):
    for _name, _n in (("g1", 16), ("bin", 112), ("caw", 64), ("cab", 16), ("br", 16), ("bi", 16),
                      ("lam", 16), ("cbw", 48), ("g2", 16)):
        _PAR[(_name, _l)] = _off
        _off += _n
_PAR[("gf", 0)] = _off
_off += 16
NS = 9
for _name in ("fr", "keep", "m0", "m1"):
    _PAR[(_name, 0)] = _off
    _off += NS
NPAR = _off
NST = 96
_DER = {}
_off = 0
for _l in range(DEPTH):
    for _name, _n in (("hbr", 16), ("hbi", 16), ("hbg", 32), ("hls", 16), ("e", 16)):
        _DER[(_name, _l)] = _off
        _off += _n
NDER = _off


class Sched:
    def __init__(self, nc, stack):
        self.nc = nc
        self.stack = stack
        self.q = {k: [] for k in ("pe", "act", "dve", "pool", "sp")}
        self.sems = {}
        self.cnt = {}
        self.hw = {k: {} for k in self.q}
        self.res = {}
        self.own = {}
        for k in ("pe", "act", "dve", "pool"):
            self.own[k] = self.new_sem("s_" + k)

    def new_sem(self, name):
        h = self.stack.enter_context(self.nc.semaphore(name))
        self.sems[name] = h
        self.cnt[name] = 0
        return name

    def op(self, q, fns, reads=(), writes=(), dma=None, sem=None):
        own = self.own.get(q)
        need = {}

        def want(tok, hazard_same_ok):
            if tok is None:
                return
            s, v = tok
            if hazard_same_ok and s == own:
                return
            if need.get(s, 0) < v:
                need[s] = v

        for r in reads:
            st = self.res.get(r)
            if st is not None:
                want(st[0], False)
        for w in writes:
            st = self.res.get(w)
            if st is not None:
                want(st[0], True)
                for t in st[1]:
                    want(t, True)
        waits = []
        hw = self.hw[q]
        for s, v in need.items():
            if hw.get(s, 0) < v:
                hw[s] = v
                waits.append((s, v))
        if sem is not None:
            self.cnt[sem] += 1
            tok = (sem, self.cnt[sem])
            inc = (sem, None)
        elif dma is not None:
            self.cnt[dma] += 16
            tok = (dma, self.cnt[dma])
            inc = (dma, 16)
        else:
            self.cnt[own] += 1
            tok = (own, self.cnt[own])
            inc = (own, 1)
        self.q[q].append((waits, fns, inc))
        for r in reads:
            st = self.res.get(r)
            if st is None:
                st = self.res[r] = [None, []]
            st[1].append(tok)
        for w in writes:
            self.res[w] = [tok, []]
        return tok

    def final_wait(self, q, toks):
        waits = []
        for s, v in toks:
            waits.append((s, v))
        self.q[q].append((waits, [], None))

    def replay(self, q, eng):
        for waits, fns, inc in self.q[q]:
            for s, v in waits:
                eng.wait_ge(self.sems[s], v)
            last = None
            for f in fns:
                last = f(eng)
            if inc is not None and last is not None:
                if inc[1] is None:
                    last.then_inc(self.sems[inc[0]])
                else:
                    last.then_inc(self.sems[inc[0]], inc[1])


class Pool_:
    def __init__(self, n):
        self.free = list(range(n))

    def get(self):
        return self.free.pop(0)

    def put(self, i):
        self.free.append(i)


def build_program(debug=False, ns=NS, n_pairs=4):
    nc = bass.Bass("TRN2", target_bir_lowering=False)
    stack = contextlib.ExitStack()
    with stack:
        xT = nc.dram_tensor("xT", [D, ns * T], F32, kind="ExternalInput").ap()
        par_d = nc.dram_tensor("par", [128, NPAR], F32, kind="ExternalInput").ap()
        w_in = nc.dram_tensor("w_in", [DEPTH, D, N_IN], F32, kind="ExternalInput").ap()
        lru_wr = nc.dram_tensor("lru_wr", [DEPTH, 8, 256, 256], F32, kind="ExternalInput").ap()
        lru_wi = nc.dram_tensor("lru_wi", [DEPTH, 8, 256, 256], F32, kind="ExternalInput").ap()
        w_pa = nc.dram_tensor("w_pa", [DEPTH, D, D], F32, kind="ExternalInput").ap()
        w_pb = nc.dram_tensor("w_pb", [DEPTH, D, D], F32, kind="ExternalInput").ap()
        w_o = nc.dram_tensor("w_o", [DEPTH, D, D], F32, kind="ExternalInput").ap()
        w_mlp1 = nc.dram_tensor("w_mlp1", [DEPTH, D, D_FF], F32, kind="ExternalInput").ap()
        w_mlp2 = nc.dram_tensor("w_mlp2", [DEPTH, D_FF, D], F32, kind="ExternalInput").ap()
        outT = nc.dram_tensor("outT", [D, ns * T], F32, kind="ExternalOutput").ap()
        scr1 = nc.dram_tensor("scr1", [DEPTH, 16, 128, 16 * 512], BF16).ap()
        scr2 = nc.dram_tensor("scr2", [DEPTH, 16, 128, 16 * 512], BF16).ap()
        cin_d = [nc.dram_tensor(f"cin{i}", [128, NST], F32).ap() for i in range(ns - 1)]
        cout_d = [nc.dram_tensor(f"cout{i}", [256, NST], F32).ap() for i in range(ns - 1)]
        dbg = nc.dram_tensor("dbg", [8, D, T], F32, kind="ExternalOutput").ap() if debug else None

        def sb(name, shape, dt):
            return stack.enter_context(nc.sbuf_tensor(name, shape, dt))

        xres = sb("xres", [128, NCH, T], F32)
        hnA = sb("hnA", [128, NCH, T], BF16)
        hnB = sb("hnB", [128, NCH, T], BF16)
        U = sb("U", [128, 32, T], BF16)
        wsl = [sb(f"wsl{i}", [128, 16, 512], BF16) for i in range(NSLOT)]
        tmp = [sb(f"tmp{i}", [128, TMPW], F32) for i in range(NTMP)]
        tb = [sb(f"tb{i}", [128, T], BF16) for i in range(NTB)]
        par = sb("par_s", [128, NPAR], F32)
        der = sb("der_s", [128, NDER], F32)
        car_in = [sb(f"car_in{i}", [128, NST], F32) for i in range(2)]
        car_out = [sb(f"car_out{i}", [128, NST], F32) for i in range(2)]
        rcv = sb("rcv", [128, 2, NST], F32)
        hnBst = hnB[:].bitcast(F32).rearrange("p (c two) t -> p c (two t)", two=2)
        ones = sb("ones", [128, 128], BF16)
        ps = stack.enter_context(nc.psum_tensor("ps", [128, 8, 512], F32))

        S = Sched(nc, stack)
        XRES_ = [("xres", c) for c in range(NCH)]
        wsem = [S.new_sem(f"s_w{i}") for i in range(NSLOT)]
        s_xin = S.new_sem("s_xin")
        s_xout = S.new_sem("s_xout")
        s_par = S.new_sem("s_par")
        s_cc = S.new_sem("s_cc")
        s_st = S.new_sem("s_st")
        s_rcv = S.new_sem("s_rcv")
        s_och = [S.new_sem(f"s_och{i}") for i in range(NCH)]
        s_sth = [S.new_sem(f"s_sth{i}") for i in range(8)]
        s_scr = [S.new_sem(f"s_scr{i}") for i in range(NSLOT)]
        s_dbg = S.new_sem("s_dbg")
        dbg_i = [0]

        def dump_xres():
            if not debug or dbg_i[0] >= 8:
                return
            k = dbg_i[0]
            dbg_i[0] += 1
            S.op("sp", [lambda e, k=k: e.dma_start(out=dbg[k].rearrange("(c p) t -> p c t", p=128), in_=xres[:])],
                 reads=XRES_, dma=s_dbg)

        tpool = Pool_(NTMP)
        tbpool = Pool_(NTB)
        slots = Pool_(NSLOT)
        bank_ctr = [0]

        reserved = set()

        def next_bank():
            while True:
                b = bank_ctr[0] % 8
                bank_ctr[0] += 1
                if b not in reserved:
                    return b

        def reserve_bank():
            b = next_bank()
            reserved.add(b)
            return b

        def release_bank(b):
            reserved.discard(b)

        def P(name, l, c, n=1):
            o = _PAR[(name, l)] + c
            return par[:, o:o + n]

        def Dv(name, l, c, n=1):
            o = _DER[(name, l)] + c
            return der[:, o:o + n]

        XRES = XRES_
        HNA = [("hnA", c) for c in range(NCH)]
        HNB = [("hnB", c) for c in range(NCH)]

        S.op("sp", [lambda e: e.dma_start(out=par[:], in_=par_d)], writes=["par"], dma=s_par)
        S.op("dve", [lambda e: e.memset(car_in[0][:], 0.0)], writes=[("carin", 0)])
        S.op("dve", [lambda e: e.memset(xres[:], 0.0)], writes=XRES_)
        S.op("dve", [lambda e: e.memset(ones[:], 1.0 / D)], writes=["ones"])
        for l in range(DEPTH):
            S.op("act", [lambda e, l=l: e.activation(out=Dv("e", l, 0, 16), in_=P("lam", l, 0, 16),
                                                     func=AF.Exp, scale=-1.0)],
                 reads=["par"], writes=[("der_e", l)])
            S.op("act", [lambda e, l=l: e.activation(out=Dv("e", l, 0, 16), in_=Dv("e", l, 0, 16),
                                                     func=AF.Ln, bias=1.0)],
                 reads=[("der_e", l)], writes=[("der_e", l)])
            S.op("act", [lambda e, l=l: e.activation(out=Dv("hls", l, 0, 16), in_=Dv("e", l, 0, 16),
                                                     func=AF.Identity, scale=-4.0)],
                 reads=[("der_e", l)], writes=["der"])
            S.op("act", [lambda e, l=l: e.activation(out=Dv("hbr", l, 0, 16), in_=P("br", l, 0, 16),
                                                     func=AF.Identity, scale=0.5)],
                 reads=["par"], writes=["der"])
            S.op("act", [lambda e, l=l: e.activation(out=Dv("hbi", l, 0, 16), in_=P("bi", l, 0, 16),
                                                     func=AF.Identity, scale=0.5)],
                 reads=["par"], writes=["der"])
            S.op("act", [lambda e, l=l: e.activation(out=Dv("hbg", l, 0, 32), in_=P("bin", l, 80, 32),
                                                     func=AF.Identity, scale=0.5)],
                 reads=["par"], writes=["der"])
        PR = ["par", "der"]

        def load_panel(src_aps):
            s = slots.get()
            for lo, hi, ap in src_aps:
                S.op("pool", [lambda e, s=s, lo=lo, hi=hi, ap=ap: e.dma_start(out=wsl[s][:, :, lo:hi], in_=ap)],
                     writes=[("w", s)], dma=wsem[s])
            return s

        def win_panel(l, col0):
            return load_panel([(0, 512, w_in[l].rearrange("(kc p) n -> p kc n", p=128)[:, :, col0:col0 + 512])])

        def sq_panel(w, l, col0):
            return load_panel([(0, 512, w[l].rearrange("(kc p) n -> p kc n", p=128)[:, :, col0:col0 + 512])])

        def cached_panel(scr, name, l, idx, sl, src_ap):
            key = ("scr", name, l, idx)
            if not CACHE_MLP_BF16:
                return load_panel([(0, 512, src_ap)])
            if sl < 2:
                s_ = load_panel([(0, 512, src_ap)])
                S.op("sp", [lambda e, s_=s_: e.dma_start(out=scr[l, idx], in_=wsl[s_][:].rearrange("p a b -> p (a b)"))],
                     reads=[("w", s_)], writes=[key], dma=s_scr[s_])
                return s_
            s_ = slots.get()
            S.op("pool", [lambda e, s_=s_: e.dma_start(out=wsl[s_][:].rearrange("p a b -> p (a b)"), in_=scr[l, idx])],
                 reads=[key], writes=[("w", s_)], dma=wsem[s_])
            return s_

        def mm_group(bank, pairs, reads):
            n = len(pairs)
            fns = []
            for i, (lt, rh) in enumerate(pairs):
                fns.append(lambda e, lt=lt, rh=rh, i=i: e.matmul(ps[:, bank, :], lhsT=lt, rhs=rh,
                                                               start=(i == 0), stop=(i == n - 1)))
            S.op("pe", fns, reads=reads, writes=[("ps", bank)])

        def proj_group(slot, cc, src, src_keys, nk=16, k0=0):
            bank = next_bank()
            pairs = [(wsl[slot][:, kc, cc * 128:(cc + 1) * 128], src[:, k0 + kc, :]) for kc in range(nk)]
            mm_group(bank, pairs, reads=[("w", slot)] + src_keys)
            return bank

        def norm(gname, gl, dst, dst_keys, inplace=False):
            bank = next_bank()
            for c in range(NCH):
                k = tbpool.get()
                S.op("act", [lambda e, c=c, k=k: e.activation(out=tb[k][:, :], in_=xres[:, c, :], func=AF.Square)],
                     reads=[("xres", c)], writes=[("tb", k)])
                S.op("pe", [lambda e, c=c, k=k: e.matmul(ps[:, bank, :], lhsT=ones[:, :], rhs=tb[k][:, :],
                                                         start=(c == 0), stop=(c == NCH - 1))],
                     reads=[("tb", k), "ones"], writes=[("ps", bank)])
                tbpool.put(k)
            r = tpool.get()
            S.op("act", [lambda e: e.activation(out=tmp[r][:, 0:T], in_=ps[:, bank, :], func=AF.Sqrt, bias=EPS)],
                 reads=[("ps", bank)], writes=[("tmp", r)])
            S.op("dve", [lambda e: e.reciprocal(out=tmp[r][:, 0:T], in_=tmp[r][:, 0:T])],
                 reads=[("tmp", r)], writes=[("tmp", r)])
            for c in range(NCH):
                o = xres[:, c, :] if inplace else dst[:, c, :]
                S.op("dve", [lambda e, c=c, o=o: e.scalar_tensor_tensor(
                    out=o, in0=xres[:, c, :], scalar=P(gname, gl, c), in1=tmp[r][:, 0:T],
                    op0=ALU.mult, op1=ALU.mult)],
                     reads=[("xres", c), ("tmp", r)] + PR, writes=[dst_keys[c]])
            tpool.put(r)

        xT_v = xT.rearrange("(c p) t -> p c t", p=128)
        stages = {}

        def stage_ap(st, c):
            return hnBst[:, c, :] if c < 8 else tmp[st["tm"][c - 8]][:, 0:T]

        def stage_keys(st, c):
            return [("hnB", 2 * c), ("hnB", 2 * c + 1)] if c < 8 else [("tmp", st["tm"][c - 8])]

        def load_stage_lo(sl):
            stages[sl] = {}
            S.op("sp", [lambda e, sl=sl: e.dma_start(out=hnBst[:, :, :], in_=xT_v[:, 0:8, sl * T:(sl + 1) * T])],
                 writes=HNB, dma=s_xin)

        def load_stage_hi(sl):
            st = stages[sl]
            st["tm"] = [tpool.get() for _ in range(8)]
            for k in range(8):
                S.op("sp", [lambda e, sl=sl, k=k, t_=st["tm"][k]: e.dma_start(
                    out=tmp[t_][:, 0:T], in_=xT_v[:, 8 + k, sl * T:(sl + 1) * T])],
                     writes=[("tmp", st["tm"][k])], dma=s_sth[k])

        def stage_stat(sl):
            st = stages[sl]
            bank = reserve_bank()
            for c in range(NCH):
                k = tbpool.get()
                sa, sk = stage_ap(st, c), stage_keys(st, c)
                S.op("act", [lambda e, sa=sa, k=k: e.activation(out=tb[k][:, :], in_=sa, func=AF.Square)],
                     reads=sk, writes=[("tb", k)])
                S.op("pe", [lambda e, c=c, k=k, bank=bank: e.matmul(ps[:, bank, :], lhsT=ones[:, :], rhs=tb[k][:, :],
                                                                    start=(c == 0), stop=(c == NCH - 1))],
                     reads=[("tb", k), "ones"], writes=[("ps", bank)])
                tbpool.put(k)
            st["mst"] = tpool.get()
            S.op("act", [lambda e, bank=bank, m=st["mst"]: e.activation(out=tmp[m][:, 0:T], in_=ps[:, bank, :],
                                                                      func=AF.Identity)],
                 reads=[("ps", bank)], writes=[("tmp", st["mst"])])
            release_bank(bank)

        def layer(l, sl):
            ci, co = car_in[sl % 2], car_out[sl % 2]
            CI, CO = ("carin", sl % 2), ("carout", sl % 2)

            G = load_panel([
                (0, 256, lru_wr[l].rearrange("n (hc p) k -> p (n hc) k", p=128)),
                (256, 512, lru_wi[l].rearrange("n (hc p) k -> p (n hc) k", p=128)),
            ])
            for jp in range(4):
                sx = win_panel(l, 0 * D + jp * 512)
                xa_t = {}
                xab_t = {}
                for cc in range(4):
                    c = jp * 4 + cc
                    bank = proj_group(sx, cc, hnA, HNA)
                    xr = tpool.get()
                    S.op("act", [lambda e, c=c, xr=xr, bank=bank: e.activation(
                        out=tmp[xr][:, 3:3 + T], in_=ps[:, bank, :], func=AF.Identity, bias=P("bin", l, c))],
                         reads=[("ps", bank)] + PR, writes=[("tmp", xr)])
                    S.op("dve", [lambda e, c=c, xr=xr: e.tensor_copy(out=tmp[xr][:, 0:3], in_=ci[:, 16 + 3 * c:19 + 3 * c])],
                         reads=[CI, ("tmp", xr)], writes=[("tmp", xr)])
                    xa = tpool.get()
                    S.op("dve", [lambda e, c=c, xr=xr, xa=xa: e.tensor_scalar(
                        out=tmp[xa][:, 0:T], in0=tmp[xr][:, 3:3 + T], scalar1=P("caw", l, 3 * 16 + c),
                        scalar2=P("cab", l, c), op0=ALU.mult, op1=ALU.add)],
                         reads=[("tmp", xr)] + PR, writes=[("tmp", xa)])
                    for k in (2, 1, 0):
                        S.op("dve", [lambda e, c=c, xr=xr, xa=xa, k=k: e.scalar_tensor_tensor(
                            out=tmp[xa][:, 0:T], in0=tmp[xr][:, k:k + T], scalar=P("caw", l, k * 16 + c),
                            in1=tmp[xa][:, 0:T], op0=ALU.mult, op1=ALU.add)],
                             reads=[("tmp", xr), ("tmp", xa)] + PR, writes=[("tmp", xa)])
                    S.op("dve", [lambda e, c=c, xr=xr: e.tensor_copy(out=co[:, 16 + 3 * c:19 + 3 * c], in_=tmp[xr][:, T:T + 3])],
                         reads=[("tmp", xr)], writes=[CO])
                    tpool.put(xr)
                    kb = tbpool.get()
                    S.op("dve", [lambda e, xa=xa, kb=kb: e.tensor_copy(out=tb[kb][:, :], in_=tmp[xa][:, 0:T])],
                         reads=[("tmp", xa)], writes=[("tb", kb)])
                    xa_t[c] = xa
                    xab_t[c] = kb
                slots.put(sx)
                sy = win_panel(l, 1 * D + jp * 512)
                gel_t = {}
                for cc in range(4):
                    c = jp * 4 + cc
                    bank = proj_group(sy, cc, hnA, HNA)
                    g = tpool.get()
                    S.op("act", [lambda e, c=c, g=g, bank=bank: e.activation(
                        out=tmp[g][:, 0:T], in_=ps[:, bank, :], func=AF.Gelu_apprx_tanh, bias=P("bin", l, 16 + c))],
                         reads=[("ps", bank)] + PR, writes=[("tmp", g)])
                    gel_t[c] = g
                slots.put(sy)
                for n in (2 * jp, 2 * jp + 1):
                    cs = (2 * n, 2 * n + 1)
                    banks = {}
                    for which, off in (("r", 0), ("i", 256)):
                        for kc in range(2):
                            bank = next_bank()
                            pairs = [(wsl[G][:, n * 2 + hc, off + kc * 128: off + (kc + 1) * 128], tb[xab_t[cs[hc]]][:, :])
                                     for hc in range(2)]
                            mm_group(bank, pairs, reads=[("w", G), ("tb", xab_t[cs[0]]), ("tb", xab_t[cs[1]])])
                            banks[(which, kc)] = bank
                    for kc in range(2):
                        c = cs[kc]
                        R = tpool.get()
                        I = tpool.get()
                        M = tpool.get()
                        br_, bi_ = banks[("r", kc)], banks[("i", kc)]
                        xa = xa_t[c]
                        g = gel_t[c]
                        S.op("act", [lambda e, c=c, R=R, br_=br_: e.activation(
                            out=tmp[R][:, 0:T], in_=ps[:, br_, :], func=AF.Tanh, scale=0.5, bias=Dv("hbr", l, c))],
                             reads=[("ps", br_)] + PR, writes=[("tmp", R)])
                        S.op("act", [lambda e, c=c, I=I, bi_=bi_: e.activation(
                            out=tmp[I][:, 0:T], in_=ps[:, bi_, :], func=AF.Tanh, scale=0.5, bias=Dv("hbi", l, c))],
                             reads=[("ps", bi_)] + PR, writes=[("tmp", I)])
                        S.op("act", [lambda e, c=c, R=R: e.activation(
                            out=tmp[R][:, 0:T], in_=tmp[R][:, 0:T], func=AF.Exp, scale=Dv("hls", l, c),
                            bias=Dv("hls", l, c))],
                             reads=[("tmp", R)] + PR, writes=[("tmp", R)])
                        S.op("act", [lambda e, R=R, M=M: e.activation(
                            out=tmp[M][:, 0:T], in_=tmp[R][:, 0:T], func=AF.Square)],
                             reads=[("tmp", R)], writes=[("tmp", M)])
                        S.op("act", [lambda e, M=M: e.activation(
                            out=tmp[M][:, 0:T], in_=tmp[M][:, 0:T], func=AF.Sqrt, scale=-0.25, bias=0.25)],
                             reads=[("tmp", M)], writes=[("tmp", M)])
                        S.op("dve", [lambda e, I=I, xa=xa: e.scalar_tensor_tensor(
                            out=tmp[I][:, 0:T], in0=tmp[I][:, 0:T], scalar=1.0, in1=tmp[xa][:, 0:T],
                            op0=ALU.add, op1=ALU.mult)],
                             reads=[("tmp", I), ("tmp", xa)], writes=[("tmp", I)])
                        S.op("dve", [lambda e, I=I, M=M: e.tensor_tensor(
                            out=tmp[I][:, 0:T], in0=tmp[I][:, 0:T], in1=tmp[M][:, 0:T], op=ALU.mult)],
                             reads=[("tmp", I), ("tmp", M)], writes=[("tmp", I)])
                        S.op("dve", [lambda e, c=c, R=R, I=I, M=M: e.tensor_tensor_scan(
                            out=tmp[M][:, 0:T], data0=tmp[R][:, 0:T], data1=tmp[I][:, 0:T],
                            initial=ci[:, c:c + 1], op0=ALU.mult, op1=ALU.add)],
                             reads=[("tmp", R), ("tmp", I), CI, ("tmp", M)], writes=[("tmp", M)])
                        S.op("dve", [lambda e, c=c, M=M: e.tensor_copy(out=co[:, c:c + 1], in_=tmp[M][:, T - 1:T])],
                             reads=[("tmp", M)], writes=[CO])
                        S.op("dve", [lambda e, c=c, M=M, g=g: e.tensor_tensor(
                            out=U[:, c, :], in0=tmp[M][:, 0:T], in1=tmp[g][:, 0:T], op=ALU.mult)],
                             reads=[("tmp", M), ("tmp", g)], writes=[("U", c)])
                        for t_ in (R, I, M, xa, g):
                            tpool.put(t_)
                        tbpool.put(xab_t[c])
            slots.put(G)

            for jp in range(4):
                sc = win_panel(l, 3 * D + jp * 512)
                ccs = {}
                for cc in range(4):
                    c = jp * 4 + cc
                    bank = proj_group(sc, cc, hnA, HNA)
                    t1 = tpool.get()
                    S.op("act", [lambda e, c=c, t1=t1, bank=bank: e.activation(
                        out=tmp[t1][:, 0:T], in_=ps[:, bank, :], func=AF.Identity, bias=P("bin", l, 48 + c))],
                         reads=[("ps", bank)] + PR, writes=[("tmp", t1)])
                    ccs[c] = t1
                slots.put(sc)
                sxx = win_panel(l, 4 * D + jp * 512)
                for cc in range(4):
                    c = jp * 4 + cc
                    bank = proj_group(sxx, cc, hnA, HNA)
                    t1 = ccs[c]
                    p = tpool.get()
                    S.op("dve", [lambda e, c=c, t1=t1, p=p, bank=bank: e.scalar_tensor_tensor(
                        out=tmp[p][:, 2:2 + T], in0=ps[:, bank, :], scalar=P("bin", l, 64 + c), in1=tmp[t1][:, 0:T],
                        op0=ALU.add, op1=ALU.mult)],
                         reads=[("ps", bank), ("tmp", t1)] + PR, writes=[("tmp", p)])
                    S.op("dve", [lambda e, c=c, p=p: e.tensor_copy(out=tmp[p][:, 0:2], in_=ci[:, 64 + 2 * c:66 + 2 * c])],
                         reads=[CI, ("tmp", p)], writes=[("tmp", p)])
                    S.op("dve", [lambda e, c=c, t1=t1, p=p: e.tensor_scalar(
                        out=tmp[t1][:, 0:T], in0=tmp[p][:, 2:2 + T], scalar1=P("cbw", l, 2 * 16 + c), scalar2=None,
                        op0=ALU.mult)],
                         reads=[("tmp", p), ("tmp", t1)] + PR, writes=[("tmp", t1)])
                    for k in (1, 0):
                        S.op("dve", [lambda e, c=c, t1=t1, p=p, k=k: e.scalar_tensor_tensor(
                            out=tmp[t1][:, 0:T], in0=tmp[p][:, k:k + T], scalar=P("cbw", l, k * 16 + c),
                            in1=tmp[t1][:, 0:T], op0=ALU.mult, op1=ALU.add)],
                             reads=[("tmp", p), ("tmp", t1)] + PR, writes=[("tmp", t1)])
                    S.op("dve", [lambda e, c=c, p=p: e.tensor_copy(out=co[:, 64 + 2 * c:66 + 2 * c], in_=tmp[p][:, T:T + 2])],
                         reads=[("tmp", p)], writes=[CO])
                    tpool.put(p)
                slots.put(sxx)
                sb_ = win_panel(l, 2 * D + jp * 512)
                for cc in range(4):
                    c = jp * 4 + cc
                    bank = proj_group(sb_, cc, hnA, HNA)
                    t1 = ccs[c]
                    S.op("dve", [lambda e, c=c, t1=t1, bank=bank: e.scalar_tensor_tensor(
                        out=U[:, 16 + c, :], in0=ps[:, bank, :], scalar=P("bin", l, 32 + c), in1=tmp[t1][:, 0:T],
                        op0=ALU.add, op1=ALU.mult)],
                         reads=[("ps", bank), ("tmp", t1)] + PR, writes=[("U", 16 + c)])
                    tpool.put(t1)
                slots.put(sb_)

            UA = [("U", c) for c in range(16)]
            UB = [("U", 16 + c) for c in range(16)]
            for jp in range(4):
                tg = {}
                for which, seg, wmat, src_k0, src_keys, hoff in (("a", 5, w_pa, 0, UA, 0), ("b", 6, w_pb, 16, UB, 16)):
                    sg = win_panel(l, seg * D + jp * 512)
                    for cc in range(4):
                        c = jp * 4 + cc
                        bank = proj_group(sg, cc, hnA, HNA)
                        t1 = tpool.get()
                        S.op("act", [lambda e, c=c, t1=t1, bank=bank, hoff=hoff: e.activation(
                            out=tmp[t1][:, 0:T], in_=ps[:, bank, :], func=AF.Tanh, scale=0.5,
                            bias=Dv("hbg", l, hoff + c))],
                             reads=[("ps", bank)] + PR, writes=[("tmp", t1)])
                        tg[(which, c)] = t1
                    slots.put(sg)
                    spj = sq_panel(wmat, l, jp * 512)
                    for cc in range(4):
                        c = jp * 4 + cc
                        bank = proj_group(spj, cc, U, src_keys, k0=src_k0)
                        t1 = tg[(which, c)]
                        S.op("dve", [lambda e, t1=t1, bank=bank: e.scalar_tensor_tensor(
                            out=tmp[t1][:, 0:T], in0=tmp[t1][:, 0:T], scalar=1.0, in1=ps[:, bank, :],
                            op0=ALU.add, op1=ALU.mult)],
                             reads=[("ps", bank), ("tmp", t1)], writes=[("tmp", t1)])
                    slots.put(spj)
                if jp == 0 and sl < ns - 1:
                    S.op("sp", [lambda e: e.dma_start(out=cin_d[sl], in_=co[:])], reads=[CO], writes=[("cin_d", sl)],
                         dma=s_st)
                    S.op("pool", [lambda e: e.collective_compute(
                        "AllGather", ALU.bypass, replica_groups=[[2 * i, 2 * i + 1] for i in range(n_pairs)],
                        ins=[cin_d[sl].opt()], outs=[cout_d[sl].opt()])],
                         reads=[("cin_d", sl)], writes=[("cout_d", sl)], sem=s_cc)
                for cc in range(4):
                    c = jp * 4 + cc
                    ta, tb_ = tg[("a", c)], tg[("b", c)]
                    S.op("dve", [lambda e, c=c, ta=ta, tb_=tb_: e.tensor_tensor(
                        out=hnB[:, c, :], in0=tmp[ta][:, 0:T], in1=tmp[tb_][:, 0:T], op=ALU.add)],
                         reads=[("tmp", ta), ("tmp", tb_)], writes=[("hnB", c)])
                    tpool.put(ta)
                    tpool.put(tb_)

            sbank = reserve_bank()
            pend = []

            def stat_chunk(c):
                k = tbpool.get()
                S.op("act", [lambda e, c=c, k=k: e.activation(out=tb[k][:, :], in_=xres[:, c, :], func=AF.Square)],
                     reads=[("xres", c)], writes=[("tb", k)])
                S.op("pe", [lambda e, c=c, k=k: e.matmul(ps[:, sbank, :], lhsT=ones[:, :], rhs=tb[k][:, :],
                                                         start=(c == 0), stop=(c == NCH - 1))],
                     reads=[("tb", k), "ones"], writes=[("ps", sbank)])
                tbpool.put(k)

            for jp in range(4):
                so = sq_panel(w_o, l, jp * 512)
                for cc in range(4):
                    c = jp * 4 + cc
                    bank = proj_group(so, cc, hnB, HNB)
                    S.op("dve", [lambda e, c=c, bank=bank: e.scalar_tensor_tensor(
                        out=xres[:, c, :], in0=ps[:, bank, :], scalar=0.5, in1=xres[:, c, :],
                        op0=ALU.mult, op1=ALU.add)],
                         reads=[("ps", bank), ("xres", c)], writes=[("xres", c)])
                    S.op("dve", [lambda e, c=c: e.tensor_scalar(
                        out=hnA[:, c, :], in0=xres[:, c, :], scalar1=P("g2", l, c), scalar2=None, op0=ALU.mult)],
                         reads=[("xres", c)] + PR, writes=[("hnA", c)])
                    pend.append(c)
                    if len(pend) > 3:
                        stat_chunk(pend.pop(0))
                slots.put(so)
            if sl + 1 < ns:
                load_stage_lo(sl + 1)

            dump_xres()
            r2 = None
            for half in range(2):
                for jp in range(8):
                    pi = half * 8 + jp
                    s1 = cached_panel(scr1, "m1", l, pi, sl, w_mlp1[l].rearrange("(kc p) n -> p kc n", p=128)
                                      [:, :, pi * 512:(pi + 1) * 512])
                    for cc in range(4):
                        uc = jp * 4 + cc
                        bank = proj_group(s1, cc, hnA, HNA)
                        t1 = tpool.get()
                        S.op("act", [lambda e, t1=t1, bank=bank: e.activation(
                            out=tmp[t1][:, 0:T], in_=ps[:, bank, :], func=AF.Relu)],
                             reads=[("ps", bank)], writes=[("tmp", t1)])
                        S.op("dve", [lambda e, uc=uc, t1=t1, bank=bank: e.tensor_tensor(
                            out=U[:, uc, :], in0=tmp[t1][:, 0:T], in1=ps[:, bank, :], op=ALU.mult)],
                             reads=[("ps", bank), ("tmp", t1)], writes=[("U", uc)])
                        tpool.put(t1)
                        if pend:
                            stat_chunk(pend.pop(0))
                    slots.put(s1)
                    if half == 0 and jp == 1:
                        r2 = tpool.get()
                        S.op("dve", [lambda e, r2=r2: e.tensor_scalar(
                            out=tmp[r2][:, 0:T], in0=ps[:, sbank, :], scalar1=EPS, scalar2=None, op0=ALU.add)],
                             reads=[("ps", sbank)], writes=[("tmp", r2)])
                        S.op("dve", [lambda e, r2=r2: e.reciprocal(out=tmp[r2][:, 0:T], in_=tmp[r2][:, 0:T])],
                             reads=[("tmp", r2)], writes=[("tmp", r2)])
                        release_bank(sbank)
                if half == 1 and sl + 1 < ns:
                    load_stage_hi(sl + 1)
                for j in range(4):
                    banks = [next_bank() for _ in range(4)]
                    for g in range(2):
                        gi = half * 2 + g
                        s2 = cached_panel(scr2, "m2", l, gi * 4 + j, sl,
                                          w_mlp2[l].rearrange("(g kc p) n -> p g kc n", p=128, kc=16)
                                          [:, gi, :, j * 512:(j + 1) * 512])
                        for m in range(4):
                            fns = []
                            for kc in range(16):
                                fns.append(lambda e, m=m, kc=kc, g=g, s2=s2, bk=banks[m]: e.matmul(
                                    ps[:, bk, :], lhsT=wsl[s2][:, kc, m * 128:(m + 1) * 128],
                                    rhs=U[:, g * 16 + kc, :], start=(g == 0 and kc == 0), stop=(g == 1 and kc == 15)))
                            S.op("pe", fns, reads=[("w", s2)] + [("U", g * 16 + kc) for kc in range(16)],
                                 writes=[("ps", banks[m])])
                        slots.put(s2)
                    for m in range(4):
                        c = j * 4 + m
                        t1 = tpool.get()
                        S.op("dve", [lambda e, t1=t1, r2=r2, bank=banks[m]: e.tensor_tensor(
                            out=tmp[t1][:, 0:T], in0=ps[:, bank, :], in1=tmp[r2][:, 0:T], op=ALU.mult)],
                             reads=[("ps", banks[m]), ("tmp", r2)], writes=[("tmp", t1)])
                        S.op("dve", [lambda e, c=c, t1=t1: e.tensor_tensor(
                            out=xres[:, c, :], in0=tmp[t1][:, 0:T], in1=xres[:, c, :], op=ALU.add)],
                             reads=[("tmp", t1), ("xres", c)], writes=[("xres", c)])
                        tpool.put(t1)
                    if half == 1 and j == 1 and sl + 1 < ns:
                        stage_stat(sl + 1)
            tpool.put(r2)

        oT_v = outT.rearrange("(c p) t -> p c t", p=128)
        out_toks = []

        def final_stat():
            bank = reserve_bank()
            for c in range(NCH):
                k = tbpool.get()
                S.op("act", [lambda e, c=c, k=k: e.activation(out=tb[k][:, :], in_=xres[:, c, :], func=AF.Square)],
                     reads=[("xres", c)], writes=[("tb", k)])
                S.op("pe", [lambda e, c=c, k=k, bank=bank: e.matmul(ps[:, bank, :], lhsT=ones[:, :], rhs=tb[k][:, :],
                                                                    start=(c == 0), stop=(c == NCH - 1))],
                     reads=[("tb", k), "ones"], writes=[("ps", bank)])
                tbpool.put(k)
            return bank

        def rstd_from(src_ap, src_keys):
            r = tpool.get()
            S.op("act", [lambda e, r=r: e.activation(out=tmp[r][:, 0:T], in_=src_ap, func=AF.Sqrt, bias=EPS)],
                 reads=src_keys, writes=[("tmp", r)])
            S.op("dve", [lambda e, r=r: e.reciprocal(out=tmp[r][:, 0:T], in_=tmp[r][:, 0:T])],
                 reads=[("tmp", r)], writes=[("tmp", r)])
            return r

        def emit_outputs(sl_prev, rf):
            for c in range(NCH):
                o = tpool.get()
                S.op("dve", [lambda e, c=c, o=o, rf=rf: e.scalar_tensor_tensor(
                    out=tmp[o][:, 0:T], in0=xres[:, c, :], scalar=P("gf", 0, c), in1=tmp[rf][:, 0:T],
                    op0=ALU.mult, op1=ALU.mult)],
                     reads=[("xres", c), ("tmp", rf)] + PR, writes=[("tmp", o)])
                tok = S.op("sp", [lambda e, c=c, o=o: e.dma_start(
                    out=oT_v[:, c, sl_prev * T:(sl_prev + 1) * T], in_=tmp[o][:, 0:T])],
                           reads=[("tmp", o)], dma=s_och[c])
                out_toks.append(tok)
                tpool.put(o)

        def boundary(sl, st):
            l = sl % 2
            bank = final_stat()
            m1 = tpool.get()
            S.op("dve", [lambda e: e.scalar_tensor_tensor(
                out=tmp[m1][:, 0:T], in0=ps[:, bank, :], scalar=P("keep", 0, sl), in1=tmp[st["mst"]][:, 0:T],
                op0=ALU.mult, op1=ALU.add)],
                 reads=[("ps", bank), ("tmp", st["mst"])] + PR, writes=[("tmp", m1)])
            S.op("act", [lambda e: e.activation(out=tmp[m1][:, 0:T], in_=tmp[m1][:, 0:T], func=AF.Sqrt, bias=EPS)],
                 reads=[("tmp", m1)], writes=[("tmp", m1)])
            S.op("dve", [lambda e: e.reciprocal(out=tmp[m1][:, 0:T], in_=tmp[m1][:, 0:T])],
                 reads=[("tmp", m1)], writes=[("tmp", m1)])
            rf = None
            if sl >= 1:
                rf = tpool.get()
                S.op("act", [lambda e: e.activation(out=tmp[rf][:, 0:T], in_=ps[:, bank, :], func=AF.Sqrt, bias=EPS)],
                     reads=[("ps", bank)], writes=[("tmp", rf)])
            release_bank(bank)
            for c in range(NCH):
                sa, sk = stage_ap(st, c), stage_keys(st, c)
                S.op("dve", [lambda e, c=c, sa=sa: e.scalar_tensor_tensor(
                    out=sa, in0=xres[:, c, :], scalar=P("keep", 0, sl), in1=sa, op0=ALU.mult, op1=ALU.add)],
                     reads=[("xres", c)] + sk + PR, writes=sk)
                S.op("dve", [lambda e, c=c, sa=sa: e.scalar_tensor_tensor(
                    out=hnA[:, c, :], in0=sa, scalar=P("g1", l, c), in1=tmp[m1][:, 0:T], op0=ALU.mult, op1=ALU.mult)],
                     reads=sk + [("tmp", m1)] + PR, writes=[("hnA", c)])
            if sl >= 1:
                S.op("dve", [lambda e: e.reciprocal(out=tmp[rf][:, 0:T], in_=tmp[rf][:, 0:T])],
                     reads=[("tmp", rf)], writes=[("tmp", rf)])
                emit_outputs(sl - 1, rf)
                tpool.put(rf)
            for c in range(NCH):
                sa, sk = stage_ap(st, c), stage_keys(st, c)
                S.op("act", [lambda e, c=c, sa=sa: e.activation(out=xres[:, c, :], in_=sa, func=AF.Identity)],
                     reads=sk, writes=[("xres", c)])
            tpool.put(m1)
            tpool.put(st["mst"])
            for t_ in st["tm"]:
                tpool.put(t_)

        load_stage_lo(0)
        load_stage_hi(0)
        stage_stat(0)
        for sl in range(ns):
            l = sl % 2
            if sl >= 1:
                for r_ in range(2):
                    S.op("sp", [lambda e, sl=sl, r_=r_: e.dma_start(out=rcv[:, r_, :],
                                                                  in_=cout_d[sl - 1][r_ * 128:(r_ + 1) * 128, :])],
                         reads=[("cout_d", sl - 1)], writes=["rcv"], dma=s_rcv)
                ci = car_in[sl % 2]
                S.op("dve", [lambda e, sl=sl, ci=ci: e.tensor_scalar(
                    out=ci[:], in0=rcv[:, 0, :], scalar1=P("m0", 0, sl), scalar2=None, op0=ALU.mult)],
                     reads=["rcv"] + PR, writes=[("carin", sl % 2)])
                S.op("dve", [lambda e, sl=sl, ci=ci: e.scalar_tensor_tensor(
                    out=ci[:], in0=rcv[:, 1, :], scalar=P("m1", 0, sl), in1=ci[:], op0=ALU.mult, op1=ALU.add)],
                     reads=["rcv", ("carin", sl % 2)] + PR, writes=[("carin", sl % 2)])
            boundary(sl, stages.pop(sl))
            layer(l, sl)
            if debug:
                dump_xres()
        bank = final_stat()
        rf = rstd_from(ps[:, bank, :], [("ps", bank)])
        release_bank(bank)
        emit_outputs(ns - 1, rf)
        tpool.put(rf)
        last = {}
        for sname, v in out_toks:
            last[sname] = max(last.get(sname, 0), v)
        S.final_wait("sp", list(last.items()))

        with nc.Block() as block:
            @block.sync
            def _(e):
                S.replay("sp", e)

            @block.tensor
            def _(e):
                S.replay("pe", e)

            @block.scalar
            def _(e):
                S.replay("act", e)

            @block.vector
            def _(e):
                S.replay("dve", e)

            @block.gpsimd
            def _(e):
                S.replay("pool", e)
    return nc


def _chunked(v, n):
    return np.ascontiguousarray(np.asarray(v, np.float32).reshape(n, 128).T)


def pack_params(inp):
    par = np.zeros((128, NPAR), np.float32)

    def put(name, l, arr):
        o = _PAR[(name, l)]
        par[:, o:o + arr.shape[1]] = arr

    for l in range(DEPTH):
        put("g1", l, _chunked(inp["norm1_g"][l], 16))
        put("bin", l, _chunked(inp["b_in"][l], 112))
        put("caw", l, np.concatenate([_chunked(inp["conv_a_w"][l][k], 16) for k in range(4)], axis=1))
        put("cab", l, _chunked(inp["conv_a_b"][l], 16))
        put("br", l, _chunked(inp["lru_br"][l].reshape(-1), 16))
        put("bi", l, _chunked(inp["lru_bi"][l].reshape(-1), 16))
        put("lam", l, _chunked(inp["lru_lam"][l], 16))
        put("cbw", l, np.concatenate([_chunked(inp["conv_b_w"][l][k], 16) for k in range(3)], axis=1))
        put("g2", l, _chunked(inp["norm2_g"][l], 16))
    put("gf", 0, _chunked(inp["final_g"], 16))
    return par


N_CORES = 8
_CACHE = {}
_WNAMES = ["w_in", "lru_wr", "lru_wi", "w_pa", "w_pb", "w_o", "w_mlp1", "w_mlp2"]
_PNAMES = ["norm1_g", "b_in", "conv_a_w", "conv_a_b", "lru_br", "lru_bi", "lru_lam", "conv_b_w", "norm2_g"]


def core_inputs(inputs, b, r, ws, ws_sw):
    x = inputs["x"]
    pin = dict(inputs)
    if r == 1:
        for k in _PNAMES:
            pin[k] = np.asarray(inputs[k])[::-1]
    par = pack_params(pin)
    for sl in range(NS):
        fresh = 1.0 if sl % 2 == r else 0.0
        par[:, _PAR[("fr", 0)] + sl] = fresh
        par[:, _PAR[("keep", 0)] + sl] = 1.0 - fresh
        if r == 0:
            par[:, _PAR[("m1", 0)] + sl] = 1.0 if sl >= 2 else 0.0
        else:
            par[:, _PAR[("m0", 0)] + sl] = 1.0 if sl >= 1 else 0.0
    xT = np.zeros((D, NS * T), np.float32)
    for sl in range(NS):
        if sl % 2 == r and sl < 8:
            xT[:, sl * T:(sl + 1) * T] = x[b, sl * T:(sl + 1) * T, :].T
    m = {"xT": xT, "par": par}
    m.update(ws_sw if r == 1 else ws)
    return m


def kernel(**inputs):
    inputs = {k: np.asarray(v, np.float32) for k, v in inputs.items()}
    x = inputs["x"]
    B, SEQ, _ = x.shape
    assert B * 2 == N_CORES and SEQ == 8 * T
    if "nc" not in _CACHE:
        _CACHE["nc"] = build_program()
    nc = _CACHE["nc"]
    ws = {k: np.ascontiguousarray(inputs[k]) for k in _WNAMES}
    ws_sw = {k: np.ascontiguousarray(inputs[k][::-1]) for k in _WNAMES}
    in_maps = [core_inputs(inputs, c // 2, c % 2, ws, ws_sw) for c in range(N_CORES)]
    res = run_bass_kernel_spmd(nc, in_maps, core_ids=list(range(N_CORES)))
    out = np.empty((B, SEQ, D), np.float32)
    for b in range(B):
        for g in range(8):
            o = res.results[2 * b + g % 2]["outT"]
            out[b, g * T:(g + 1) * T, :] = o[:, (g + 1) * T:(g + 2) * T].T
    return out
```

```python
import contextlib
import numpy as np
import concourse.bass as bass
import concourse.mybir as mybir
from concourse.bass_utils import run_bass_kernel_spmd

F32 = mybir.dt.float32
BF16 = mybir.dt.bfloat16
AF = mybir.ActivationFunctionType
ALU = mybir.AluOpType

D = 2048
NCH = 16
T = 512
DEPTH = 2
N_IN = 14336
D_FF = 8192
EPS = 1e-6
NSLOT = 4
CACHE_MLP_BF16 = True
NTMP = 18
NTB = 5
TMPW = 516

_PAR = {}
_off = 0
for _l in range(DEPTH):
    for _name, _n in (("g1", 16), ("bin", 112), ("caw", 64), ("cab", 16), ("br", 16), ("bi", 16),
                      ("lam", 16), ("cbw", 48), ("g2", 16)):
        _PAR[(_name, _l)] = _off
        _off += _n
_PAR[("gf", 0)] = _off
_off += 16
NS = 9
for _name in ("fr", "keep", "m0", "m1"):
    _PAR[(_name, 0)] = _off
    _off += NS
NPAR = _off
NST = 96
_DER = {}
_off = 0
for _l in range(DEPTH):
    for _name, _n in (("hbr", 16), ("hbi", 16), ("hbg", 32), ("hls", 16), ("e", 16)):
        _DER[(_name, _l)] = _off
        _off += _n
NDER = _off


class Sched:
    def __init__(self, nc, stack):
        self.nc = nc
        self.stack = stack
        self.q = {k: [] for k in ("pe", "act", "dve", "pool", "sp")}
        self.sems = {}
        self.cnt = {}
        self.hw = {k: {} for k in self.q}
        self.res = {}
        self.own = {}
        for k in ("pe", "act", "dve", "pool"):
            self.own[k] = self.new_sem("s_" + k)

    def new_sem(self, name):
        h = self.stack.enter_context(self.nc.semaphore(name))
        self.sems[name] = h
        self.cnt[name] = 0
        return name

    def op(self, q, fns, reads=(), writes=(), dma=None, sem=None):
        own = self.own.get(q)
        need = {}

        def want(tok, hazard_same_ok):
            if tok is None:
                return
            s, v = tok
            if hazard_same_ok and s == own:
                return
            if need.get(s, 0) < v:
                need[s] = v

        for r in reads:
            st = self.res.get(r)
            if st is not None:
                want(st[0], False)
        for w in writes:
            st = self.res.get(w)
            if st is not None:
                want(st[0], True)
                for t in st[1]:
                    want(t, True)
        waits = []
        hw = self.hw[q]
        for s, v in need.items():
            if hw.get(s, 0) < v:
                hw[s] = v
                waits.append((s, v))
        if sem is not None:
            self.cnt[sem] += 1
            tok = (sem, self.cnt[sem])
            inc = (sem, None)
        elif dma is not None:
            self.cnt[dma] += 16
            tok = (dma, self.cnt[dma])
            inc = (dma, 16)
        else:
            self.cnt[own] += 1
            tok = (own, self.cnt[own])
            inc = (own, 1)
        self.q[q].append((waits, fns, inc))
        for r in reads:
            st = self.res.get(r)
            if st is None:
                st = self.res[r] = [None, []]
            st[1].append(tok)
        for w in writes:
            self.res[w] = [tok, []]
        return tok

    def final_wait(self, q, toks):
        waits = []
        for s, v in toks:
            waits.append((s, v))
        self.q[q].append((waits, [], None))

    def replay(self, q, eng):
        for waits, fns, inc in self.q[q]:
            for s, v in waits:
                eng.wait_ge(self.sems[s], v)
            last = None
            for f in fns:
                last = f(eng)
            if inc is not None and last is not None:
                if inc[1] is None:
                    last.then_inc(self.sems[inc[0]])
                else:
                    last.then_inc(self.sems[inc[0]], inc[1])


class Pool_:
    def __init__(self, n):
        self.free = list(range(n))

    def get(self):
        return self.free.pop(0)

    def put(self, i):
        self.free.append(i)


def build_program(debug=False, ns=NS, n_pairs=4):
    nc = bass.Bass("TRN2", target_bir_lowering=False)
    stack = contextlib.ExitStack()
    with stack:
        xT = nc.dram_tensor("xT", [D, ns * T], F32, kind="ExternalInput").ap()
        par_d = nc.dram_tensor("par", [128, NPAR], F32, kind="ExternalInput").ap()
        w_in = nc.dram_tensor("w_in", [DEPTH, D, N_IN], F32, kind="ExternalInput").ap()
        lru_wr = nc.dram_tensor("lru_wr", [DEPTH, 8, 256, 256], F32, kind="ExternalInput").ap()
        lru_wi = nc.dram_tensor("lru_wi", [DEPTH, 8, 256, 256], F32, kind="ExternalInput").ap()
        w_pa = nc.dram_tensor("w_pa", [DEPTH, D, D], F32, kind="ExternalInput").ap()
        w_pb = nc.dram_tensor("w_pb", [DEPTH, D, D], F32, kind="ExternalInput").ap()
        w_o = nc.dram_tensor("w_o", [DEPTH, D, D], F32, kind="ExternalInput").ap()
        w_mlp1 = nc.dram_tensor("w_mlp1", [DEPTH, D, D_FF], F32, kind="ExternalInput").ap()
        w_mlp2 = nc.dram_tensor("w_mlp2", [DEPTH, D_FF, D], F32, kind="ExternalInput").ap()
        outT = nc.dram_tensor("outT", [D, ns * T], F32, kind="ExternalOutput").ap()
        scr1 = nc.dram_tensor("scr1", [DEPTH, 16, 128, 16 * 512], BF16).ap()
        scr2 = nc.dram_tensor("scr2", [DEPTH, 16, 128, 16 * 512], BF16).ap()
        cin_d = [nc.dram_tensor(f"cin{i}", [128, NST], F32).ap() for i in range(ns - 1)]
        cout_d = [nc.dram_tensor(f"cout{i}", [256, NST], F32).ap() for i in range(ns - 1)]
        dbg = nc.dram_tensor("dbg", [8, D, T], F32, kind="ExternalOutput").ap() if debug else None

        def sb(name, shape, dt):
            return stack.enter_context(nc.sbuf_tensor(name, shape, dt))

        xres = sb("xres", [128, NCH, T], F32)
        hnA = sb("hnA", [128, NCH, T], BF16)
        hnB = sb("hnB", [128, NCH, T], BF16)
        U = sb("U", [128, 32, T], BF16)
        wsl = [sb(f"wsl{i}", [128, 16, 512], BF16) for i in range(NSLOT)]
        tmp = [sb(f"tmp{i}", [128, TMPW], F32) for i in range(NTMP)]
        tb = [sb(f"tb{i}", [128, T], BF16) for i in range(NTB)]
        par = sb("par_s", [128, NPAR], F32)
        der = sb("der_s", [128, NDER], F32)
        car_in = [sb(f"car_in{i}", [128, NST], F32) for i in range(2)]
        car_out = [sb(f"car_out{i}", [128, NST], F32) for i in range(2)]
        rcv = sb("rcv", [128, 2, NST], F32)
        hnBst = hnB[:].bitcast(F32).rearrange("p (c two) t -> p c (two t)", two=2)
        ones = sb("ones", [128, 128], BF16)
        ps = stack.enter_context(nc.psum_tensor("ps", [128, 8, 512], F32))

        S = Sched(nc, stack)
        XRES_ = [("xres", c) for c in range(NCH)]
        wsem = [S.new_sem(f"s_w{i}") for i in range(NSLOT)]
        s_xin = S.new_sem("s_xin")
        s_xout = S.new_sem("s_xout")
        s_par = S.new_sem("s_par")
        s_cc = S.new_sem("s_cc")
        s_st = S.new_sem("s_st")
        s_rcv = S.new_sem("s_rcv")
        s_och = [S.new_sem(f"s_och{i}") for i in range(NCH)]
        s_sth = [S.new_sem(f"s_sth{i}") for i in range(8)]
        s_scr = [S.new_sem(f"s_scr{i}") for i in range(NSLOT)]
        s_dbg = S.new_sem("s_dbg")
        dbg_i = [0]

        def dump_xres():
            if not debug or dbg_i[0] >= 8:
                return
            k = dbg_i[0]
            dbg_i[0] += 1
            S.op("sp", [lambda e, k=k: e.dma_start(out=dbg[k].rearrange("(c p) t -> p c t", p=128), in_=xres[:])],
                 reads=XRES_, dma=s_dbg)

        tpool = Pool_(NTMP)
        tbpool = Pool_(NTB)
        slots = Pool_(NSLOT)
        bank_ctr = [0]

        reserved = set()

        def next_bank():
            while True:
                b = bank_ctr[0] % 8
                bank_ctr[0] += 1
                if b not in reserved:
                    return b

        def reserve_bank():
            b = next_bank()
            reserved.add(b)
            return b

        def release_bank(b):
            reserved.discard(b)

        def P(name, l, c, n=1):
            o = _PAR[(name, l)] + c
            return par[:, o:o + n]

        def Dv(name, l, c, n=1):
            o = _DER[(name, l)] + c
            return der[:, o:o + n]

        XRES = XRES_
        HNA = [("hnA", c) for c in range(NCH)]
        HNB = [("hnB", c) for c in range(NCH)]

        S.op("sp", [lambda e: e.dma_start(out=par[:], in_=par_d)], writes=["par"], dma=s_par)
        S.op("dve", [lambda e: e.memset(car_in[0][:], 0.0)], writes=[("carin", 0)])
        S.op("dve", [lambda e: e.memset(xres[:], 0.0)], writes=XRES_)
        S.op("dve", [lambda e: e.memset(ones[:], 1.0 / D)], writes=["ones"])
        for l in range(DEPTH):
            S.op("act", [lambda e, l=l: e.activation(out=Dv("e", l, 0, 16), in_=P("lam", l, 0, 16),
                                                     func=AF.Exp, scale=-1.0)],
                 reads=["par"], writes=[("der_e", l)])
            S.op("act", [lambda e, l=l: e.activation(out=Dv("e", l, 0, 16), in_=Dv("e", l, 0, 16),
                                                     func=AF.Ln, bias=1.0)],
                 reads=[("der_e", l)], writes=[("der_e", l)])
            S.op("act", [lambda e, l=l: e.activation(out=Dv("hls", l, 0, 16), in_=Dv("e", l, 0, 16),
                                                     func=AF.Identity, scale=-4.0)],
                 reads=[("der_e", l)], writes=["der"])
            S.op("act", [lambda e, l=l: e.activation(out=Dv("hbr", l, 0, 16), in_=P("br", l, 0, 16),
                                                     func=AF.Identity, scale=0.5)],
                 reads=["par"], writes=["der"])
            S.op("act", [lambda e, l=l: e.activation(out=Dv("hbi", l, 0, 16), in_=P("bi", l, 0, 16),
                                                     func=AF.Identity, scale=0.5)],
                 reads=["par"], writes=["der"])
            S.op("act", [lambda e, l=l: e.activation(out=Dv("hbg", l, 0, 32), in_=P("bin", l, 80, 32),
                                                     func=AF.Identity, scale=0.5)],
                 reads=["par"], writes=["der"])
        PR = ["par", "der"]

        def load_panel(src_aps):
            s = slots.get()
            for lo, hi, ap in src_aps:
                S.op("pool", [lambda e, s=s, lo=lo, hi=hi, ap=ap: e.dma_start(out=wsl[s][:, :, lo:hi], in_=ap)],
                     writes=[("w", s)], dma=wsem[s])
            return s

        def win_panel(l, col0):
            return load_panel([(0, 512, w_in[l].rearrange("(kc p) n -> p kc n", p=128)[:, :, col0:col0 + 512])])

        def sq_panel(w, l, col0):
            return load_panel([(0, 512, w[l].rearrange("(kc p) n -> p kc n", p=128)[:, :, col0:col0 + 512])])

        def cached_panel(scr, name, l, idx, sl, src_ap):
            key = ("scr", name, l, idx)
            if not (CACHE_MLP_BF16 and name == "m2"):
                return load_panel([(0, 512, src_ap)])
            if sl < 2:
                s_ = load_panel([(0, 512, src_ap)])
                S.op("sp", [lambda e, s_=s_: e.dma_start(out=scr[l, idx], in_=wsl[s_][:].rearrange("p a b -> p (a b)"))],
                     reads=[("w", s_)], writes=[key], dma=s_scr[s_])
                return s_
            s_ = slots.get()
            S.op("pool", [lambda e, s_=s_: e.dma_start(out=wsl[s_][:].rearrange("p a b -> p (a b)"), in_=scr[l, idx])],
                 reads=[key], writes=[("w", s_)], dma=wsem[s_])
            return s_

        def mm_group(bank, pairs, reads):
            n = len(pairs)
            fns = []
            for i, (lt, rh) in enumerate(pairs):
                fns.append(lambda e, lt=lt, rh=rh, i=i: e.matmul(ps[:, bank, :], lhsT=lt, rhs=rh,
                                                               start=(i == 0), stop=(i == n - 1)))
            S.op("pe", fns, reads=reads, writes=[("ps", bank)])

        def proj_group(slot, cc, src, src_keys, nk=16, k0=0):
            bank = next_bank()
            pairs = [(wsl[slot][:, kc, cc * 128:(cc + 1) * 128], src[:, k0 + kc, :]) for kc in range(nk)]
            mm_group(bank, pairs, reads=[("w", slot)] + src_keys)
            return bank

        def norm(gname, gl, dst, dst_keys, inplace=False):
            bank = next_bank()
            for c in range(NCH):
                k = tbpool.get()
                S.op("act", [lambda e, c=c, k=k: e.activation(out=tb[k][:, :], in_=xres[:, c, :], func=AF.Square)],
                     reads=[("xres", c)], writes=[("tb", k)])
                S.op("pe", [lambda e, c=c, k=k: e.matmul(ps[:, bank, :], lhsT=ones[:, :], rhs=tb[k][:, :],
                                                         start=(c == 0), stop=(c == NCH - 1))],
                     reads=[("tb", k), "ones"], writes=[("ps", bank)])
                tbpool.put(k)
            r = tpool.get()
            S.op("act", [lambda e: e.activation(out=tmp[r][:, 0:T], in_=ps[:, bank, :], func=AF.Sqrt, bias=EPS)],
                 reads=[("ps", bank)], writes=[("tmp", r)])
            S.op("dve", [lambda e: e.reciprocal(out=tmp[r][:, 0:T], in_=tmp[r][:, 0:T])],
                 reads=[("tmp", r)], writes=[("tmp", r)])
            for c in range(NCH):
                o = xres[:, c, :] if inplace else dst[:, c, :]
                S.op("dve", [lambda e, c=c, o=o: e.scalar_tensor_tensor(
                    out=o, in0=xres[:, c, :], scalar=P(gname, gl, c), in1=tmp[r][:, 0:T],
                    op0=ALU.mult, op1=ALU.mult)],
                     reads=[("xres", c), ("tmp", r)] + PR, writes=[dst_keys[c]])
            tpool.put(r)

        xT_v = xT.rearrange("(c p) t -> p c t", p=128)
        stages = {}

        def stage_ap(st, c):
            return hnBst[:, c, :] if c < 8 else tmp[st["tm"][c - 8]][:, 0:T]

        def stage_keys(st, c):
            return [("hnB", 2 * c), ("hnB", 2 * c + 1)] if c < 8 else [("tmp", st["tm"][c - 8])]

        def load_stage_lo(sl):
            stages[sl] = {}
            S.op("sp", [lambda e, sl=sl: e.dma_start(out=hnBst[:, :, :], in_=xT_v[:, 0:8, sl * T:(sl + 1) * T])],
                 writes=HNB, dma=s_xin)

        def load_stage_hi(sl):
            st = stages[sl]
            st["tm"] = [tpool.get() for _ in range(8)]
            for k in range(8):
                S.op("sp", [lambda e, sl=sl, k=k, t_=st["tm"][k]: e.dma_start(
                    out=tmp[t_][:, 0:T], in_=xT_v[:, 8 + k, sl * T:(sl + 1) * T])],
                     writes=[("tmp", st["tm"][k])], dma=s_sth[k])

        def stage_stat(sl):
            st = stages[sl]
            bank = reserve_bank()
            for c in range(NCH):
                k = tbpool.get()
                sa, sk = stage_ap(st, c), stage_keys(st, c)
                S.op("act", [lambda e, sa=sa, k=k: e.activation(out=tb[k][:, :], in_=sa, func=AF.Square)],
                     reads=sk, writes=[("tb", k)])
                S.op("pe", [lambda e, c=c, k=k, bank=bank: e.matmul(ps[:, bank, :], lhsT=ones[:, :], rhs=tb[k][:, :],
                                                                    start=(c == 0), stop=(c == NCH - 1))],
                     reads=[("tb", k), "ones"], writes=[("ps", bank)])
                tbpool.put(k)
            st["mst"] = tpool.get()
            S.op("act", [lambda e, bank=bank, m=st["mst"]: e.activation(out=tmp[m][:, 0:T], in_=ps[:, bank, :],
                                                                      func=AF.Identity)],
                 reads=[("ps", bank)], writes=[("tmp", st["mst"])])
            release_bank(bank)

        def layer(l, sl):
            ci, co = car_in[sl % 2], car_out[sl % 2]
            CI, CO = ("carin", sl % 2), ("carout", sl % 2)

            G = load_panel([
                (0, 256, lru_wr[l].rearrange("n (hc p) k -> p (n hc) k", p=128)),
                (256, 512, lru_wi[l].rearrange("n (hc p) k -> p (n hc) k", p=128)),
            ])
            for jp in range(4):
                sx = win_panel(l, 0 * D + jp * 512)
                xa_t = {}
                xab_t = {}
                for cc in range(4):
                    c = jp * 4 + cc
                    bank = proj_group(sx, cc, hnA, HNA)
                    xr = tpool.get()
                    S.op("act", [lambda e, c=c, xr=xr, bank=bank: e.activation(
                        out=tmp[xr][:, 3:3 + T], in_=ps[:, bank, :], func=AF.Identity, bias=P("bin", l, c))],
                         reads=[("ps", bank)] + PR, writes=[("tmp", xr)])
                    S.op("dve", [lambda e, c=c, xr=xr: e.tensor_copy(out=tmp[xr][:, 0:3], in_=ci[:, 16 + 3 * c:19 + 3 * c])],
                         reads=[CI, ("tmp", xr)], writes=[("tmp", xr)])
                    xa = tpool.get()
                    S.op("dve", [lambda e, c=c, xr=xr, xa=xa: e.tensor_scalar(
                        out=tmp[xa][:, 0:T], in0=tmp[xr][:, 3:3 + T], scalar1=P("caw", l, 3 * 16 + c),
                        scalar2=P("cab", l, c), op0=ALU.mult, op1=ALU.add)],
                         reads=[("tmp", xr)] + PR, writes=[("tmp", xa)])
                    for k in (2, 1, 0):
                        S.op("dve", [lambda e, c=c, xr=xr, xa=xa, k=k: e.scalar_tensor_tensor(
                            out=tmp[xa][:, 0:T], in0=tmp[xr][:, k:k + T], scalar=P("caw", l, k * 16 + c),
                            in1=tmp[xa][:, 0:T], op0=ALU.mult, op1=ALU.add)],
                             reads=[("tmp", xr), ("tmp", xa)] + PR, writes=[("tmp", xa)])
                    S.op("dve", [lambda e, c=c, xr=xr: e.tensor_copy(out=co[:, 16 + 3 * c:19 + 3 * c], in_=tmp[xr][:, T:T + 3])],
                         reads=[("tmp", xr)], writes=[CO])
                    tpool.put(xr)
                    kb = tbpool.get()
                    S.op("dve", [lambda e, xa=xa, kb=kb: e.tensor_copy(out=tb[kb][:, :], in_=tmp[xa][:, 0:T])],
                         reads=[("tmp", xa)], writes=[("tb", kb)])
                    xa_t[c] = xa
                    xab_t[c] = kb
                slots.put(sx)
                sy = win_panel(l, 1 * D + jp * 512)
                gel_t = {}
                for cc in range(4):
                    c = jp * 4 + cc
                    bank = proj_group(sy, cc, hnA, HNA)
                    g = tpool.get()
                    S.op("act", [lambda e, c=c, g=g, bank=bank: e.activation(
                        out=tmp[g][:, 0:T], in_=ps[:, bank, :], func=AF.Gelu_apprx_tanh, bias=P("bin", l, 16 + c))],
                         reads=[("ps", bank)] + PR, writes=[("tmp", g)])
                    gel_t[c] = g
                slots.put(sy)
                for n in (2 * jp, 2 * jp + 1):
                    cs = (2 * n, 2 * n + 1)
                    banks = {}
                    for which, off in (("r", 0), ("i", 256)):
                        for kc in range(2):
                            bank = next_bank()
                            pairs = [(wsl[G][:, n * 2 + hc, off + kc * 128: off + (kc + 1) * 128], tb[xab_t[cs[hc]]][:, :])
                                     for hc in range(2)]
                            mm_group(bank, pairs, reads=[("w", G), ("tb", xab_t[cs[0]]), ("tb", xab_t[cs[1]])])
                            banks[(which, kc)] = bank
                    for kc in range(2):
                        c = cs[kc]
                        R = tpool.get()
                        I = tpool.get()
                        M = tpool.get()
                        br_, bi_ = banks[("r", kc)], banks[("i", kc)]
                        xa = xa_t[c]
                        g = gel_t[c]
                        S.op("act", [lambda e, c=c, R=R, br_=br_: e.activation(
                            out=tmp[R][:, 0:T], in_=ps[:, br_, :], func=AF.Tanh, scale=0.5, bias=Dv("hbr", l, c))],
                             reads=[("ps", br_)] + PR, writes=[("tmp", R)])
                        S.op("act", [lambda e, c=c, I=I, bi_=bi_: e.activation(
                            out=tmp[I][:, 0:T], in_=ps[:, bi_, :], func=AF.Tanh, scale=0.5, bias=Dv("hbi", l, c))],
                             reads=[("ps", bi_)] + PR, writes=[("tmp", I)])
                        S.op("act", [lambda e, c=c, R=R: e.activation(
                            out=tmp[R][:, 0:T], in_=tmp[R][:, 0:T], func=AF.Exp, scale=Dv("hls", l, c),
                            bias=Dv("hls", l, c))],
                             reads=[("tmp", R)] + PR, writes=[("tmp", R)])
                        S.op("act", [lambda e, R=R, M=M: e.activation(
                            out=tmp[M][:, 0:T], in_=tmp[R][:, 0:T], func=AF.Square)],
                             reads=[("tmp", R)], writes=[("tmp", M)])
                        S.op("act", [lambda e, M=M: e.activation(
                            out=tmp[M][:, 0:T], in_=tmp[M][:, 0:T], func=AF.Sqrt, scale=-0.25, bias=0.25)],
                             reads=[("tmp", M)], writes=[("tmp", M)])
                        S.op("dve", [lambda e, I=I, xa=xa: e.scalar_tensor_tensor(
                            out=tmp[I][:, 0:T], in0=tmp[I][:, 0:T], scalar=1.0, in1=tmp[xa][:, 0:T],
                            op0=ALU.add, op1=ALU.mult)],
                             reads=[("tmp", I), ("tmp", xa)], writes=[("tmp", I)])
                        S.op("dve", [lambda e, I=I, M=M: e.tensor_tensor(
                            out=tmp[I][:, 0:T], in0=tmp[I][:, 0:T], in1=tmp[M][:, 0:T], op=ALU.mult)],
                             reads=[("tmp", I), ("tmp", M)], writes=[("tmp", I)])
                        S.op("dve", [lambda e, c=c, R=R, I=I, M=M: e.tensor_tensor_scan(
                            out=tmp[M][:, 0:T], data0=tmp[R][:, 0:T], data1=tmp[I][:, 0:T],
                            initial=ci[:, c:c + 1], op0=ALU.mult, op1=ALU.add)],
                             reads=[("tmp", R), ("tmp", I), CI, ("tmp", M)], writes=[("tmp", M)])
                        S.op("dve", [lambda e, c=c, M=M: e.tensor_copy(out=co[:, c:c + 1], in_=tmp[M][:, T - 1:T])],
                             reads=[("tmp", M)], writes=[CO])
                        S.op("dve", [lambda e, c=c, M=M, g=g: e.tensor_tensor(
                            out=U[:, c, :], in0=tmp[M][:, 0:T], in1=tmp[g][:, 0:T], op=ALU.mult)],
                             reads=[("tmp", M), ("tmp", g)], writes=[("U", c)])
                        for t_ in (R, I, M, xa, g):
                            tpool.put(t_)
                        tbpool.put(xab_t[c])
            slots.put(G)

            for jp in range(4):
                sc = win_panel(l, 3 * D + jp * 512)
                ccs = {}
                for cc in range(4):
                    c = jp * 4 + cc
                    bank = proj_group(sc, cc, hnA, HNA)
                    t1 = tpool.get()
                    S.op("act", [lambda e, c=c, t1=t1, bank=bank: e.activation(
                        out=tmp[t1][:, 0:T], in_=ps[:, bank, :], func=AF.Identity, bias=P("bin", l, 48 + c))],
                         reads=[("ps", bank)] + PR, writes=[("tmp", t1)])
                    ccs[c] = t1
                slots.put(sc)
                sxx = win_panel(l, 4 * D + jp * 512)
                for cc in range(4):
                    c = jp * 4 + cc
                    bank = proj_group(sxx, cc, hnA, HNA)
                    t1 = ccs[c]
                    p = tpool.get()
                    S.op("dve", [lambda e, c=c, t1=t1, p=p, bank=bank: e.scalar_tensor_tensor(
                        out=tmp[p][:, 2:2 + T], in0=ps[:, bank, :], scalar=P("bin", l, 64 + c), in1=tmp[t1][:, 0:T],
                        op0=ALU.add, op1=ALU.mult)],
                         reads=[("ps", bank), ("tmp", t1)] + PR, writes=[("tmp", p)])
                    S.op("dve", [lambda e, c=c, p=p: e.tensor_copy(out=tmp[p][:, 0:2], in_=ci[:, 64 + 2 * c:66 + 2 * c])],
                         reads=[CI, ("tmp", p)], writes=[("tmp", p)])
                    S.op("dve", [lambda e, c=c, t1=t1, p=p: e.tensor_scalar(
                        out=tmp[t1][:, 0:T], in0=tmp[p][:, 2:2 + T], scalar1=P("cbw", l, 2 * 16 + c), scalar2=None,
                        op0=ALU.mult)],
                         reads=[("tmp", p), ("tmp", t1)] + PR, writes=[("tmp", t1)])
                    for k in (1, 0):
                        S.op("dve", [lambda e, c=c, t1=t1, p=p, k=k: e.scalar_tensor_tensor(
                            out=tmp[t1][:, 0:T], in0=tmp[p][:, k:k + T], scalar=P("cbw", l, k * 16 + c),
                            in1=tmp[t1][:, 0:T], op0=ALU.mult, op1=ALU.add)],
                             reads=[("tmp", p), ("tmp", t1)] + PR, writes=[("tmp", t1)])
                    S.op("dve", [lambda e, c=c, p=p: e.tensor_copy(out=co[:, 64 + 2 * c:66 + 2 * c], in_=tmp[p][:, T:T + 2])],
                         reads=[("tmp", p)], writes=[CO])
                    tpool.put(p)
                slots.put(sxx)
                sb_ = win_panel(l, 2 * D + jp * 512)
                for cc in range(4):
                    c = jp * 4 + cc
                    bank = proj_group(sb_, cc, hnA, HNA)
                    t1 = ccs[c]
                    S.op("dve", [lambda e, c=c, t1=t1, bank=bank: e.scalar_tensor_tensor(
                        out=U[:, 16 + c, :], in0=ps[:, bank, :], scalar=P("bin", l, 32 + c), in1=tmp[t1][:, 0:T],
                        op0=ALU.add, op1=ALU.mult)],
                         reads=[("ps", bank), ("tmp", t1)] + PR, writes=[("U", 16 + c)])
                    tpool.put(t1)
                slots.put(sb_)

            UA = [("U", c) for c in range(16)]
            UB = [("U", 16 + c) for c in range(16)]
            for jp in range(4):
                tg = {}
                for which, seg, wmat, src_k0, src_keys, hoff in (("a", 5, w_pa, 0, UA, 0), ("b", 6, w_pb, 16, UB, 16)):
                    sg = win_panel(l, seg * D + jp * 512)
                    for cc in range(4):
                        c = jp * 4 + cc
                        bank = proj_group(sg, cc, hnA, HNA)
                        t1 = tpool.get()
                        S.op("act", [lambda e, c=c, t1=t1, bank=bank, hoff=hoff: e.activation(
                            out=tmp[t1][:, 0:T], in_=ps[:, bank, :], func=AF.Tanh, scale=0.5,
                            bias=Dv("hbg", l, hoff + c))],
                             reads=[("ps", bank)] + PR, writes=[("tmp", t1)])
                        tg[(which, c)] = t1
                    slots.put(sg)
                    spj = sq_panel(wmat, l, jp * 512)
                    for cc in range(4):
                        c = jp * 4 + cc
                        bank = proj_group(spj, cc, U, src_keys, k0=src_k0)
                        t1 = tg[(which, c)]
                        S.op("dve", [lambda e, t1=t1, bank=bank: e.scalar_tensor_tensor(
                            out=tmp[t1][:, 0:T], in0=tmp[t1][:, 0:T], scalar=1.0, in1=ps[:, bank, :],
                            op0=ALU.add, op1=ALU.mult)],
                             reads=[("ps", bank), ("tmp", t1)], writes=[("tmp", t1)])
                    slots.put(spj)
                if jp == 0 and sl < ns - 1:
                    S.op("sp", [lambda e: e.dma_start(out=cin_d[sl], in_=co[:])], reads=[CO], writes=[("cin_d", sl)],
                         dma=s_st)
                    S.op("pool", [lambda e: e.collective_compute(
                        "AllGather", ALU.bypass, replica_groups=[[2 * i, 2 * i + 1] for i in range(n_pairs)],
                        ins=[cin_d[sl].opt()], outs=[cout_d[sl].opt()])],
                         reads=[("cin_d", sl)], writes=[("cout_d", sl)], sem=s_cc)
                for cc in range(4):
                    c = jp * 4 + cc
                    ta, tb_ = tg[("a", c)], tg[("b", c)]
                    S.op("dve", [lambda e, c=c, ta=ta, tb_=tb_: e.tensor_tensor(
                        out=hnB[:, c, :], in0=tmp[ta][:, 0:T], in1=tmp[tb_][:, 0:T], op=ALU.add)],
                         reads=[("tmp", ta), ("tmp", tb_)], writes=[("hnB", c)])
                    tpool.put(ta)
                    tpool.put(tb_)

            sbank = reserve_bank()
            pend = []

            def stat_chunk(c):
                k = tbpool.get()
                S.op("act", [lambda e, c=c, k=k: e.activation(out=tb[k][:, :], in_=xres[:, c, :], func=AF.Square)],
                     reads=[("xres", c)], writes=[("tb", k)])
                S.op("pe", [lambda e, c=c, k=k: e.matmul(ps[:, sbank, :], lhsT=ones[:, :], rhs=tb[k][:, :],
                                                         start=(c == 0), stop=(c == NCH - 1))],
                     reads=[("tb", k), "ones"], writes=[("ps", sbank)])
                tbpool.put(k)

            for jp in range(4):
                so = sq_panel(w_o, l, jp * 512)
                for cc in range(4):
                    c = jp * 4 + cc
                    bank = proj_group(so, cc, hnB, HNB)
                    S.op("dve", [lambda e, c=c, bank=bank: e.scalar_tensor_tensor(
                        out=xres[:, c, :], in0=ps[:, bank, :], scalar=0.5, in1=xres[:, c, :],
                        op0=ALU.mult, op1=ALU.add)],
                         reads=[("ps", bank), ("xres", c)], writes=[("xres", c)])
                    S.op("dve", [lambda e, c=c: e.tensor_scalar(
                        out=hnA[:, c, :], in0=xres[:, c, :], scalar1=P("g2", l, c), scalar2=None, op0=ALU.mult)],
                         reads=[("xres", c)] + PR, writes=[("hnA", c)])
                    pend.append(c)
                    if len(pend) > 3:
                        stat_chunk(pend.pop(0))
                slots.put(so)
            if sl + 1 < ns:
                load_stage_lo(sl + 1)

            dump_xres()
            r2 = None
            for half in range(2):
                for jp in range(8):
                    pi = half * 8 + jp
                    s1 = cached_panel(scr1, "m1", l, pi, sl, w_mlp1[l].rearrange("(kc p) n -> p kc n", p=128)
                                      [:, :, pi * 512:(pi + 1) * 512])
                    for cc in range(4):
                        uc = jp * 4 + cc
                        bank = proj_group(s1, cc, hnA, HNA)
                        t1 = tpool.get()
                        S.op("act", [lambda e, t1=t1, bank=bank: e.activation(
                            out=tmp[t1][:, 0:T], in_=ps[:, bank, :], func=AF.Relu)],
                             reads=[("ps", bank)], writes=[("tmp", t1)])
                        S.op("dve", [lambda e, uc=uc, t1=t1, bank=bank: e.tensor_tensor(
                            out=U[:, uc, :], in0=tmp[t1][:, 0:T], in1=ps[:, bank, :], op=ALU.mult)],
                             reads=[("ps", bank), ("tmp", t1)], writes=[("U", uc)])
                        tpool.put(t1)
                        if pend:
                            stat_chunk(pend.pop(0))
                    slots.put(s1)
                    if half == 0 and jp == 1:
                        r2 = tpool.get()
                        S.op("dve", [lambda e, r2=r2: e.tensor_scalar(
                            out=tmp[r2][:, 0:T], in0=ps[:, sbank, :], scalar1=EPS, scalar2=None, op0=ALU.add)],
                             reads=[("ps", sbank)], writes=[("tmp", r2)])
                        S.op("dve", [lambda e, r2=r2: e.reciprocal(out=tmp[r2][:, 0:T], in_=tmp[r2][:, 0:T])],
                             reads=[("tmp", r2)], writes=[("tmp", r2)])
                        release_bank(sbank)
                if half == 1 and sl + 1 < ns:
                    load_stage_hi(sl + 1)
                for j in range(4):
                    banks = [next_bank() for _ in range(4)]
                    for g in range(2):
                        gi = half * 2 + g
                        s2 = cached_panel(scr2, "m2", l, gi * 4 + j, sl,
                                          w_mlp2[l].rearrange("(g kc p) n -> p g kc n", p=128, kc=16)
                                          [:, gi, :, j * 512:(j + 1) * 512])
                        for m in range(4):
                            fns = []
                            for kc in range(16):
                                fns.append(lambda e, m=m, kc=kc, g=g, s2=s2, bk=banks[m]: e.matmul(
                                    ps[:, bk, :], lhsT=wsl[s2][:, kc, m * 128:(m + 1) * 128],
                                    rhs=U[:, g * 16 + kc, :], start=(g == 0 and kc == 0), stop=(g == 1 and kc == 15)))
                            S.op("pe", fns, reads=[("w", s2)] + [("U", g * 16 + kc) for kc in range(16)],
                                 writes=[("ps", banks[m])])
                        slots.put(s2)
                    for m in range(4):
                        c = j * 4 + m
                        t1 = tpool.get()
                        S.op("dve", [lambda e, t1=t1, r2=r2, bank=banks[m]: e.tensor_tensor(
                            out=tmp[t1][:, 0:T], in0=ps[:, bank, :], in1=tmp[r2][:, 0:T], op=ALU.mult)],
                             reads=[("ps", banks[m]), ("tmp", r2)], writes=[("tmp", t1)])
                        S.op("dve", [lambda e, c=c, t1=t1: e.tensor_tensor(
                            out=xres[:, c, :], in0=tmp[t1][:, 0:T], in1=xres[:, c, :], op=ALU.add)],
                             reads=[("tmp", t1), ("xres", c)], writes=[("xres", c)])
                        tpool.put(t1)
                    if half == 1 and j == 1 and sl + 1 < ns:
                        stage_stat(sl + 1)
            tpool.put(r2)

        oT_v = outT.rearrange("(c p) t -> p c t", p=128)
        out_toks = []

        def final_stat():
            bank = reserve_bank()
            for c in range(NCH):
                k = tbpool.get()
                S.op("act", [lambda e, c=c, k=k: e.activation(out=tb[k][:, :], in_=xres[:, c, :], func=AF.Square)],
                     reads=[("xres", c)], writes=[("tb", k)])
                S.op("pe", [lambda e, c=c, k=k, bank=bank: e.matmul(ps[:, bank, :], lhsT=ones[:, :], rhs=tb[k][:, :],
                                                                    start=(c == 0), stop=(c == NCH - 1))],
                     reads=[("tb", k), "ones"], writes=[("ps", bank)])
                tbpool.put(k)
            return bank

        def rstd_from(src_ap, src_keys):
            r = tpool.get()
            S.op("act", [lambda e, r=r: e.activation(out=tmp[r][:, 0:T], in_=src_ap, func=AF.Sqrt, bias=EPS)],
                 reads=src_keys, writes=[("tmp", r)])
            S.op("dve", [lambda e, r=r: e.reciprocal(out=tmp[r][:, 0:T], in_=tmp[r][:, 0:T])],
                 reads=[("tmp", r)], writes=[("tmp", r)])
            return r

        def emit_outputs(sl_prev, rf):
            for c in range(NCH):
                o = tpool.get()
                S.op("dve", [lambda e, c=c, o=o, rf=rf: e.scalar_tensor_tensor(
                    out=tmp[o][:, 0:T], in0=xres[:, c, :], scalar=P("gf", 0, c), in1=tmp[rf][:, 0:T],
                    op0=ALU.mult, op1=ALU.mult)],
                     reads=[("xres", c), ("tmp", rf)] + PR, writes=[("tmp", o)])
                tok = S.op("sp", [lambda e, c=c, o=o: e.dma_start(
                    out=oT_v[:, c, sl_prev * T:(sl_prev + 1) * T], in_=tmp[o][:, 0:T])],
                           reads=[("tmp", o)], dma=s_och[c])
                out_toks.append(tok)
                tpool.put(o)

        def boundary(sl, st):
            l = sl % 2
            bank = final_stat()
            m1 = tpool.get()
            S.op("dve", [lambda e: e.scalar_tensor_tensor(
                out=tmp[m1][:, 0:T], in0=ps[:, bank, :], scalar=P("keep", 0, sl), in1=tmp[st["mst"]][:, 0:T],
                op0=ALU.mult, op1=ALU.add)],
                 reads=[("ps", bank), ("tmp", st["mst"])] + PR, writes=[("tmp", m1)])
            S.op("act", [lambda e: e.activation(out=tmp[m1][:, 0:T], in_=tmp[m1][:, 0:T], func=AF.Sqrt, bias=EPS)],
                 reads=[("tmp", m1)], writes=[("tmp", m1)])
            S.op("dve", [lambda e: e.reciprocal(out=tmp[m1][:, 0:T], in_=tmp[m1][:, 0:T])],
                 reads=[("tmp", m1)], writes=[("tmp", m1)])
            rf = None
            if sl >= 1:
                rf = tpool.get()
                S.op("act", [lambda e: e.activation(out=tmp[rf][:, 0:T], in_=ps[:, bank, :], func=AF.Sqrt, bias=EPS)],
                     reads=[("ps", bank)], writes=[("tmp", rf)])
            release_bank(bank)
            for c in range(NCH):
                sa, sk = stage_ap(st, c), stage_keys(st, c)
                S.op("dve", [lambda e, c=c, sa=sa: e.scalar_tensor_tensor(
                    out=sa, in0=xres[:, c, :], scalar=P("keep", 0, sl), in1=sa, op0=ALU.mult, op1=ALU.add)],
                     reads=[("xres", c)] + sk + PR, writes=sk)
                S.op("dve", [lambda e, c=c, sa=sa: e.scalar_tensor_tensor(
                    out=hnA[:, c, :], in0=sa, scalar=P("g1", l, c), in1=tmp[m1][:, 0:T], op0=ALU.mult, op1=ALU.mult)],
                     reads=sk + [("tmp", m1)] + PR, writes=[("hnA", c)])
            if sl >= 1:
                S.op("dve", [lambda e: e.reciprocal(out=tmp[rf][:, 0:T], in_=tmp[rf][:, 0:T])],
                     reads=[("tmp", rf)], writes=[("tmp", rf)])
                emit_outputs(sl - 1, rf)
                tpool.put(rf)
            for c in range(NCH):
                sa, sk = stage_ap(st, c), stage_keys(st, c)
                S.op("act", [lambda e, c=c, sa=sa: e.activation(out=xres[:, c, :], in_=sa, func=AF.Identity)],
                     reads=sk, writes=[("xres", c)])
            tpool.put(m1)
            tpool.put(st["mst"])
            for t_ in st["tm"]:
                tpool.put(t_)

        load_stage_lo(0)
        load_stage_hi(0)
        stage_stat(0)
        for sl in range(ns):
            l = sl % 2
            if sl >= 1:
                for r_ in range(2):
                    S.op("sp", [lambda e, sl=sl, r_=r_: e.dma_start(out=rcv[:, r_, :],
                                                                  in_=cout_d[sl - 1][r_ * 128:(r_ + 1) * 128, :])],
                         reads=[("cout_d", sl - 1)], writes=["rcv"], dma=s_rcv)
                ci = car_in[sl % 2]
                S.op("dve", [lambda e, sl=sl, ci=ci: e.tensor_scalar(
                    out=ci[:], in0=rcv[:, 0, :], scalar1=P("m0", 0, sl), scalar2=None, op0=ALU.mult)],
                     reads=["rcv"] + PR, writes=[("carin", sl % 2)])
                S.op("dve", [lambda e, sl=sl, ci=ci: e.scalar_tensor_tensor(
                    out=ci[:], in0=rcv[:, 1, :], scalar=P("m1", 0, sl), in1=ci[:], op0=ALU.mult, op1=ALU.add)],
                     reads=["rcv", ("carin", sl % 2)] + PR, writes=[("carin", sl % 2)])
            boundary(sl, stages.pop(sl))
            layer(l, sl)
            if debug:
                dump_xres()
        bank = final_stat()
        rf = rstd_from(ps[:, bank, :], [("ps", bank)])
        release_bank(bank)
        emit_outputs(ns - 1, rf)
        tpool.put(rf)
        last = {}
        for sname, v in out_toks:
            last[sname] = max(last.get(sname, 0), v)
        S.final_wait("sp", list(last.items()))

        with nc.Block() as block:
            @block.sync
            def _(e):
                S.replay("sp", e)

            @block.tensor
            def _(e):
                S.replay("pe", e)

            @block.scalar
            def _(e):
                S.replay("act", e)

            @block.vector
            def _(e):
                S.replay("dve", e)

            @block.gpsimd
            def _(e):
                S.replay("pool", e)
    return nc


def _chunked(v, n):
    return np.ascontiguousarray(np.asarray(v, np.float32).reshape(n, 128).T)


def pack_params(inp):
    par = np.zeros((128, NPAR), np.float32)

    def put(name, l, arr):
        o = _PAR[(name, l)]
        par[:, o:o + arr.shape[1]] = arr

    for l in range(DEPTH):
        put("g1", l, _chunked(inp["norm1_g"][l], 16))
        put("bin", l, _chunked(inp["b_in"][l], 112))
        put("caw", l, np.concatenate([_chunked(inp["conv_a_w"][l][k], 16) for k in range(4)], axis=1))
        put("cab", l, _chunked(inp["conv_a_b"][l], 16))
        put("br", l, _chunked(inp["lru_br"][l].reshape(-1), 16))
        put("bi", l, _chunked(inp["lru_bi"][l].reshape(-1), 16))
        put("lam", l, _chunked(inp["lru_lam"][l], 16))
        put("cbw", l, np.concatenate([_chunked(inp["conv_b_w"][l][k], 16) for k in range(3)], axis=1))
        put("g2", l, _chunked(inp["norm2_g"][l], 16))
    put("gf", 0, _chunked(inp["final_g"], 16))
    return par


N_CORES = 8
_CACHE = {}
_WNAMES = ["w_in", "lru_wr", "lru_wi", "w_pa", "w_pb", "w_o", "w_mlp1", "w_mlp2"]
_PNAMES = ["norm1_g", "b_in", "conv_a_w", "conv_a_b", "lru_br", "lru_bi", "lru_lam", "conv_b_w", "norm2_g"]


def core_inputs(inputs, b, r, ws, ws_sw):
    x = inputs["x"]
    pin = dict(inputs)
    if r == 1:
        for k in _PNAMES:
            pin[k] = np.asarray(inputs[k])[::-1]
    par = pack_params(pin)
    for sl in range(NS):
        fresh = 1.0 if sl % 2 == r else 0.0
        par[:, _PAR[("fr", 0)] + sl] = fresh
        par[:, _PAR[("keep", 0)] + sl] = 1.0 - fresh
        if r == 0:
            par[:, _PAR[("m1", 0)] + sl] = 1.0 if sl >= 2 else 0.0
        else:
            par[:, _PAR[("m0", 0)] + sl] = 1.0 if sl >= 1 else 0.0
    xT = np.zeros((D, NS * T), np.float32)
    for sl in range(NS):
        if sl % 2 == r and sl < 8:
            xT[:, sl * T:(sl + 1) * T] = x[b, sl * T:(sl + 1) * T, :].T
    m = {"xT": xT, "par": par}
    m.update(ws_sw if r == 1 else ws)
    return m


def kernel(**inputs):
    inputs = {k: np.asarray(v, np.float32) for k, v in inputs.items()}
    x = inputs["x"]
    B, SEQ, _ = x.shape
    assert B * 2 == N_CORES and SEQ == 8 * T
    if "nc" not in _CACHE:
        _CACHE["nc"] = build_program()
    nc = _CACHE["nc"]
    ws = {k: np.ascontiguousarray(inputs[k]) for k in _WNAMES}
    ws_sw = {k: np.ascontiguousarray(inputs[k][::-1]) for k in _WNAMES}
    in_maps = [core_inputs(inputs, c // 2, c % 2, ws, ws_sw) for c in range(N_CORES)]
    res = run_bass_kernel_spmd(nc, in_maps, core_ids=list(range(N_CORES)))
    out = np.empty((B, SEQ, D), np.float32)
    for b in range(B):
        for g in range(8):
            o = res.results[2 * b + g % 2]["outT"]
            out[b, g * T:(g + 1) * T, :] = o[:, (g + 1) * T:(g + 2) * T].T
    return out
```
